# Optimizing a Trainium2 kernel written in Bass

```python
import jax, jax.numpy as jnp
from jax import lax
import numpy as np

D_MODEL = 1024
BATCH = 8
SEQ = 2048
DEPTH = 1

MIX_WIDTH = D_MODEL
CONV_WIDTH = MIX_WIDTH // 2
CONV_HEADS = 8
CONV_KERNEL = 31
POOL_WIDTH = MIX_WIDTH - CONV_WIDTH
POOL_WINDOWS = (2, 4, 8, 16)
POOL_GROUPS = len(POOL_WINDOWS)
POOL_GROUP_DIM = POOL_WIDTH // POOL_GROUPS
IN_WIDTH = 2 * CONV_WIDTH + POOL_WIDTH
D_FF = ((8 * D_MODEL // 3 + 255) // 256) * 256
N_MOD = 6
EPS = 1e-6

kernel_name = "hybrid_conv_pool_adaln_block"


def rmsnorm(x, g):
    xf = x.astype(jnp.float32)
    y = xf * lax.rsqrt(jnp.mean(xf * xf, axis=-1, keepdims=True) + EPS)
    return (y * g.astype(jnp.float32)).astype(x.dtype)


def layernorm(x, g, b):
    xf = x.astype(jnp.float32)
    mu = jnp.mean(xf, axis=-1, keepdims=True)
    var = jnp.mean(jnp.square(xf - mu), axis=-1, keepdims=True)
    y = (xf - mu) * lax.rsqrt(var + EPS)
    return (y * g.astype(jnp.float32) + b.astype(jnp.float32)).astype(x.dtype)


def conv_mixer(u, dw_w, dw_b, ln_g, ln_b, w_pw):
    a, g = jnp.split(u, 2, axis=-1)
    h = a * jax.nn.sigmoid(g)
    h = lax.conv_general_dilated(
        h, dw_w[:, None, :].astype(h.dtype), window_strides=(1,),
        padding=[(CONV_KERNEL - 1, 0)],
        dimension_numbers=('NWC', 'WIO', 'NWC'),
        feature_group_count=CONV_WIDTH) + dw_b
    h = layernorm(h, ln_g, ln_b)
    h = jax.nn.silu(h)
    return h @ w_pw


def pool_mixer(v, w_group, scale):
    B, S, _ = v.shape
    vg = v.astype(jnp.float32).reshape(B, S, POOL_GROUPS, POOL_GROUP_DIM)
    cs = jnp.cumsum(vg, axis=1)
    t = jnp.arange(S)
    pooled = []
    for gi, w in enumerate(POOL_WINDOWS):
        c_g = cs[:, :, gi]
        shifted = jnp.pad(c_g, ((0, 0), (w, 0), (0, 0)))[:, :S]
        cnt = jnp.minimum(t + 1, w).astype(jnp.float32)
        pooled.append((c_g - shifted) / cnt[None, :, None])
    p = (jnp.stack(pooled, axis=2) - vg).astype(v.dtype)
    y = jnp.einsum('bsgc,gcd->bsgd', p, w_group).reshape(B, S, POOL_WIDTH)
    return y * scale


def setup_inputs(seed: int = 0) -> dict:
    key = jax.random.key(seed)
    ks = jax.random.split(key, 20)
    L, D = DEPTH, D_MODEL
    n = lambda k, shp, s: jax.random.normal(k, shp, jnp.float32) * s
    return {
        "x": n(ks[0], (BATCH, SEQ, D), 1.0),
        "c": n(ks[1], (BATCH, D), 1.0),
        "w_ada": n(ks[2], (L, D, N_MOD * D), 0.3 * D ** -0.5),
        "b_ada": n(ks[3], (L, N_MOD * D), 0.01),
        "g_norm1": 1.0 + n(ks[4], (L, D), 0.05),
        "w_in": n(ks[5], (L, D, IN_WIDTH), D ** -0.5),
        "dw_w": n(ks[6], (L, CONV_KERNEL, CONV_WIDTH), CONV_KERNEL ** -0.5),
        "dw_b": n(ks[7], (L, CONV_WIDTH), 0.01),
        "conv_ln_g": 1.0 + n(ks[8], (L, CONV_WIDTH), 0.05),
        "conv_ln_b": n(ks[9], (L, CONV_WIDTH), 0.01),
        "w_conv_pw": n(ks[10], (L, CONV_WIDTH, CONV_WIDTH), CONV_WIDTH ** -0.5),
        "w_pool_group": n(ks[11], (L, POOL_GROUPS, POOL_GROUP_DIM, POOL_GROUP_DIM), POOL_GROUP_DIM ** -0.5),
        "pool_scale": 1.0 + n(ks[12], (L, POOL_WIDTH), 0.1),
        "w_out": n(ks[13], (L, MIX_WIDTH, D), MIX_WIDTH ** -0.5),
        "g_norm2": 1.0 + n(ks[14], (L, D), 0.05),
        "w_ffn_gate": n(ks[15], (L, D, D_FF), D ** -0.5),
        "w_ffn_up": n(ks[16], (L, D, D_FF), D ** -0.5),
        "w_ffn_down": n(ks[17], (L, D_FF, D), D_FF ** -0.5),
        "g_final": 1.0 + n(ks[18], (D,), 0.05),
    }


def reference(x, c, w_ada, b_ada, g_norm1, w_in, dw_w, dw_b, conv_ln_g, conv_ln_b,
              w_conv_pw, w_pool_group, pool_scale, w_out, g_norm2,
              w_ffn_gate, w_ffn_up, w_ffn_down, g_final):
    c_act = jax.nn.silu(c)
    for l in range(DEPTH):
        mod = c_act @ w_ada[l] + b_ada[l]
        sh1, sc1, gt1, sh2, sc2, gt2 = [m[:, None, :] for m in jnp.split(mod, N_MOD, axis=-1)]

        h = rmsnorm(x, g_norm1[l]) * (1 + sc1) + sh1
        u = h @ w_in[l]
        u_conv = u[..., :2 * CONV_WIDTH]
        u_pool = u[..., 2 * CONV_WIDTH:]
        y_conv = conv_mixer(u_conv, dw_w[l], dw_b[l], conv_ln_g[l], conv_ln_b[l], w_conv_pw[l])
        y_pool = pool_mixer(u_pool, w_pool_group[l], pool_scale[l])
        y = jnp.concatenate([y_conv, y_pool], axis=-1) @ w_out[l]
        x = x + gt1 * y

        h = rmsnorm(x, g_norm2[l]) * (1 + sc2) + sh2
        f = (jax.nn.silu(h @ w_ffn_gate[l]) * (h @ w_ffn_up[l])) @ w_ffn_down[l]
        x = x + gt2 * f
    return rmsnorm(x, g_final)
```

```python
import contextlib
import numpy as np
import concourse.bass as bass
import concourse.mybir as mybir
from concourse.bass_utils import run_bass_kernel_spmd

F32 = mybir.dt.float32
BF16 = mybir.dt.bfloat16
U8 = mybir.dt.uint8
AF = mybir.ActivationFunctionType
ALU = mybir.AluOpType
AX = mybir.AxisListType

D = 1024
SEQ = 2048
NT = 16
NST = 4
CW = 512
KW = 31
DFF = 2816
NJ = 22
EPS = 1e-6
GROUPS = [[0, 1, 2, 3, 4, 5], [6, 7, 8, 9, 10, 11], [12, 13, 14, 15, 16], [17, 18, 19, 20, 21]]
GMAX = 6
SYNC_SAME_ENGINE_WAW = True


class Op:
    __slots__ = ("eng", "fn", "is_dma", "deps", "idx", "ev", "need_ev", "semkey")


class Sched:
    ENGS = ("pe", "act", "dve", "pool", "sp")

    def __init__(self, nc):
        self.nc = nc
        self.ops = []
        self.last_w = {}
        self.readers = {}
        self.last_dma_by_key = {}
        self.buf_fence = {}

    def add(self, eng, fn, reads=(), writes=(), dma=None):
        op = Op()
        op.eng = eng
        op.fn = fn
        op.is_dma = dma is not None
        op.semkey = dma
        op.idx = len(self.ops)
        op.ev = None
        op.need_ev = False
        deps = {}

        def dep(o, raw):
            if o is not op:
                deps[o] = deps.get(o, False) or raw

        def lastw(k):
            w = self.last_w.get(k)
            if w is None:
                w = self.buf_fence.get(k[0])
            return w

        for r in reads:
            w = lastw(r)
            if w is not None:
                dep(w, True)
        for w_ in writes:
            w = lastw(w_)
            if w is not None:
                dep(w, SYNC_SAME_ENGINE_WAW)
            rd = self.readers.get(w_)
            if rd:
                for o in rd.values():
                    dep(o, SYNC_SAME_ENGINE_WAW)
        if op.is_dma:
            prev = self.last_dma_by_key.get(dma)
            if prev is not None:
                dep(prev, True)
            self.last_dma_by_key[dma] = op
        for w_ in writes:
            self.last_w[w_] = op
            self.readers[w_] = {}
        for r in reads:
            d = self.readers.setdefault(r, {})
            if op.is_dma:
                d[("dma", op.idx)] = op
            else:
                d[eng] = op
        final = []
        for o, raw in deps.items():
            if (not o.is_dma) and (not op.is_dma) and o.eng == eng:
                if eng == "pe" or not raw:
                    continue
            final.append(o)
            o.need_ev = True
        op.deps = final
        self.ops.append(op)
        return op

    def fence(self, eng, fn, old_bufs, new_bufs):
        olds = set(old_bufs)
        keys = [k for k in set(self.last_w) | set(self.readers) if k[0] in olds]
        op = self.add(eng, fn, reads=(), writes=keys)
        for b in new_bufs:
            self.buf_fence[b] = op
        return op

    def emit(self, finish_ops=()):
        nc = self.nc
        with contextlib.ExitStack() as st:
            eng_sem = {e: st.enter_context(nc.semaphore("s_" + e)) for e in self.ENGS}
            dma_keys = []
            seen = set()
            for op in self.ops:
                if op.is_dma and op.semkey not in seen:
                    seen.add(op.semkey)
                    dma_keys.append(op.semkey)
            dma_sem = {k: st.enter_context(nc.semaphore("d%d" % i)) for i, k in enumerate(dma_keys)}
            eng_cnt = {e: 0 for e in self.ENGS}
            dma_cnt = {k: 0 for k in dma_keys}
            per_eng = {e: [] for e in self.ENGS}
            for op in self.ops:
                per_eng[op.eng].append(op)
                if op.is_dma:
                    dma_cnt[op.semkey] += 16
                    op.ev = (dma_sem[op.semkey], dma_cnt[op.semkey], 16)
                elif op.need_ev:
                    eng_cnt[op.eng] += 1
                    op.ev = (eng_sem[op.eng], eng_cnt[op.eng], 1)
            fin_waits = [o.ev for o in finish_ops]
            block = st.enter_context(nc.Block())

            def run(e_handle, ename):
                known = {}
                for op in per_eng[ename]:
                    for d in op.deps:
                        sem, val, _ = d.ev
                        k = id(sem)
                        if known.get(k, 0) >= val:
                            continue
                        e_handle.wait_ge(sem, val)
                        known[k] = val
                    ins = op.fn(e_handle)
                    if op.ev is not None:
                        ins.then_inc(op.ev[0], op.ev[2])
                if ename == "sp":
                    for sem, val, _ in fin_waits:
                        if known.get(id(sem), 0) >= val:
                            continue
                        e_handle.wait_ge(sem, val)
                        known[id(sem)] = val

            @block.tensor
            def _(e):
                run(e, "pe")

            @block.scalar
            def _(e):
                run(e, "act")

            @block.vector
            def _(e):
                run(e, "dve")

            @block.gpsimd
            def _(e):
                run(e, "pool")

            @block.sync
            def _(e):
                run(e, "sp")
        return {"n_ops": len(self.ops), "eng_cnt": eng_cnt, "n_dma_sems": len(dma_keys)}


class Arena:
    def __init__(self, ar, limit):
        self.ar = ar
        self.limit = limit
        self.items = []

    def raw(self, name, off, nbytes, p0, p1):
        assert off % 32 == 0, (name, off)
        assert off + nbytes <= self.limit, (name, off, nbytes, self.limit)
        for (n2, o2, b2, q0, q1) in self.items:
            if not (p1 < q0 or q1 < p0):
                assert off + nbytes <= o2 or o2 + b2 <= off, ("overlap", name, n2)
        self.items.append((name, off, nbytes, p0, p1))
        return self.ar[:, off:off + nbytes]

    def f32(self, name, off, n, p0, p1):
        return self.raw(name, off, n * 4, p0, p1).bitcast(F32)

    def bf(self, name, off, n, p0, p1):
        return self.raw(name, off, n * 2, p0, p1).bitcast(BF16)


def build_program():
    nc = bass.Bass("TRN2", target_bir_lowering=False)
    dt_in = lambda name, shape: nc.dram_tensor(name, list(shape), F32, kind="ExternalInput").ap()
    x_d = dt_in("x", [SEQ, D])
    vec_d = dt_in("vecs", [128, 40])
    dwwt_d = dt_in("dwwt", [128, 124])
    w_ada = dt_in("w_ada", [D, 6 * D])
    b_ada = dt_in("b_ada", [1, 6 * D])
    w_in = dt_in("w_in", [D, 3 * CW])
    w_pw = dt_in("w_conv_pw", [CW, CW])
    w_pg = dt_in("w_pool_group", [4, 128, 128])
    w_out = dt_in("w_out", [D, D])
    w_g = dt_in("w_ffn_gate", [D, DFF])
    w_u = dt_in("w_ffn_up", [D, DFF])
    w_d = dt_in("w_ffn_down", [DFF, D])
    gf_d = dt_in("g_final", [1, D])
    out_d = nc.dram_tensor("out", [SEQ, D], F32, kind="ExternalOutput").ap()

    K = 1024
    LIMIT = 207 * K + 512
    with contextlib.ExitStack() as st:
        ar = st.enter_context(nc.sbuf_tensor("arena", [128, LIMIT], U8))
        banks = [st.enter_context(nc.psum_tensor("bank%d" % i, [128, 512], F32)) for i in range(8)]
        A = Arena(ar, LIMIT)
        o = 0
        IDF = A.f32("IDF", o, 128, 0, 9); o += 512
        ONES = A.f32("ONES", o, 128, 0, 9); o += 512
        IDB = A.bf("IDB", o, 128, 0, 9); o += 256
        VEC = A.f32("VEC", o, 64, 0, 9); o += 256
        DWWT = A.f32("DWWT", o, 128, 0, 9); o += 512
        CACT = A.bf("CACT", o, 16, 0, 9); o += 32
        CACTF = A.f32("CACTF", o, 8, 0, 9); o += 32
        EPSC = A.f32("EPSC", o, 8, 0, 9); o += 32
        FJ = A.f32("FJ", o, 32, 0, 9); o += 128
        AB = A.f32("AB", o, 32, 0, 9); o += 128
        SS = A.f32("SS", o, 48, 0, 9); o += 192
        RS = A.f32("RS", o, 48, 0, 9); o += 192
        NEGH = A.f32("NEGH", o, 512, 0, 9); o += 2048
        RCNT = A.f32("RCNT", o, 64, 0, 9); o += 256
        JUNK = A.bf("JUNK", o, 1024, 0, 9); o += 2048
        GT1B_OFF = o
        GT1B = A.f32("GT1B", o, 1024, 0, 2); o += 4096
        GT2B = A.f32("GT2B", o, 1024, 0, 9); o += 4096
        GFB = A.f32("GFB", o, 1024, 0, 9); o += 4096
        assert o <= 20 * K, o
        RX = 20 * K
        RH = RX + 64 * K
        RY = RH + 33280
        RW = RY + 33 * K + 512
        SS3 = SS.rearrange("p (a b) -> p a b", a=3)
        RS3 = RS.rearrange("p (a b) -> p a b", a=3)
        RCNT3 = RCNT.rearrange("p (a b) -> p a b", a=4)
        o = RX
        WIN = A.bf("WIN", o, 8 * 1536, 0, 1).rearrange("p (k n) -> p k n", k=8); o += 24 * K
        HT = [A.bf("HT%d" % i, o + i * 8 * K, 8 * 512, 1, 1).rearrange("p (k n) -> p k n", k=8) for i in range(2)]; o += 16 * K
        NXIN = 3
        XIN = [A.f32("XIN%d" % i, o + i * 4 * K, 1024, 0, 1) for i in range(NXIN)]; o += NXIN * 4 * K
        HB = [A.bf("H%d" % i, o + i * 2 * K, 1024, 1, 1) for i in range(2)]; o += 4 * K
        SIG = [A.f32("SIG%d" % i, o + i * 2 * K, 512, 1, 1) for i in range(1)]; o += 2 * K
        T12 = [A.f32("T%d" % i, o + i * 2112, 528, 1, 1) for i in range(2)]; o += 4224 + 32 * 0
        o = (o + 31) // 32 * 32
        assert o <= RH, (o, RH)
        GLW = 30 + SEQ + 2
        GLU = A.bf("GLU", RH, 4 * GLW, 0, 2).rearrange("p (c n) -> p c n", c=4)
        PP = A.bf("PP", RH + 4 * GLW * 2, 4 * SEQ, 1, 2).rearrange("p (c n) -> p c n", c=4)
        assert 4 * GLW * 2 + 4 * SEQ * 2 <= 33280
        o = RY
        VV = [A.f32("V%d" % i, o + i * 8448, 4 * 528, 1, 1).rearrange("p (c n) -> p c n", c=4) for i in range(2)]; o += 2 * 8448
        WA = [A.bf("WA%d" % i, o + i * 4 * K, 8 * 256, 0, 1).rearrange("p (k n) -> p k n", k=8) for i in range(4)]; o += 16 * K
        assert o <= RW, (o, RW)
        o = RW
        DIAG = A.bf("DIAG", o, 124 * 128, 0, 2).rearrange("p (j n) -> p j n", j=124); o += 31 * K
        WPW = A.bf("WPW", o, 4 * 512, 0, 2).rearrange("p (k n) -> p k n", k=4); o += 4 * K
        WPG = A.bf("WPG", o, 4 * 128, 0, 2).rearrange("p (k n) -> p k n", k=4); o += 1 * K
        RWT = o
        MODROW = A.f32("MODROW", o, 1024, 0, 1); o += 4 * K
        BADA = A.f32("BADA", o, 1024, 0, 1); o += 4 * K
        GFROW = A.f32("GFROW", o, 1024, 0, 1); o += 4 * K
        BADA1 = GFROW
        RBUF = [A.f32("RBUF%d" % i, o + i * 2112, 528, 1, 1) for i in range(4)]; o += 8448
        assert o <= LIMIT
        o = RX
        CONV = [A.f32("CONV%d" % i, o + i * 8 * K, 4 * 512, 2, 2).rearrange("p (c n) -> p c n", c=4) for i in range(2)]; o += 16 * K
        SQ = A.f32("SQ", o, 4 * 512, 2, 2).rearrange("p (c n) -> p c n", c=4); o += 8 * K
        MUB = A.f32("MUB", o, 512, 2, 2); o += 2 * K
        RSB = A.f32("RSB", o, 512, 2, 2); o += 2 * K
        TMB = A.f32("TMB", o, 512, 2, 2); o += 2 * K
        LT = [A.f32("LT%d" % i, o + i * 2 * K, 512, 2, 2) for i in range(2)]; o += 4 * K
        ZT = [A.bf("ZT%d" % i, o + i * 4 * K, 4 * 512, 2, 2).rearrange("p (c n) -> p c n", c=4) for i in range(2)]; o += 8 * K
        WA2 = [A.bf("WA2%d" % i, o + i * 4 * K, 8 * 256, 2, 2).rearrange("p (k n) -> p k n", k=8) for i in range(3)]; o += 12 * K
        MODROW2 = A.f32("MODROW2", o, 1024, 2, 2); o += 4 * K
        assert o <= RX + 60 * K, o
        YT = A.bf("YT", RY, 8 * SEQ, 2, 3).rearrange("p (k n) -> p k n", k=8)
        X = A.f32("X", RX, NT * D, 3, 4).rearrange("p (t d) -> p t d", t=NT)
        H2T = A.bf("H2T", RH, 8 * SEQ, 3, 4).rearrange("p (k n) -> p k n", k=8)
        o = RWT
        WOUT = A.bf("WOUT", o, 8 * D, 2, 3).rearrange("p (k n) -> p k n", k=8); o += 16 * K
        WOST = [A.f32("WOST%d" % i, o + i * 2 * K, 512, 2, 3) for i in range(2)]; o += 4 * K
        H2B = [A.bf("H2B%d" % i, GT1B_OFF + i * 2 * K, 1024, 3, 3) for i in range(2)]
        assert o <= LIMIT, (o, LIMIT)
        ACT_ = A.bf("A", RY, GMAX * SEQ, 4, 4).rearrange("p (j n) -> p j n", j=GMAX)
        o = RW
        WG = [A.bf("WG%d" % i, o + i * 2 * K, 8 * 128, 3, 4).rearrange("p (k n) -> p k n", k=8) for i in range(3)]; o += 6 * K
        WU = [A.bf("WU%d" % i, o + i * 2 * K, 8 * 128, 3, 4).rearrange("p (k n) -> p k n", k=8) for i in range(3)]; o += 6 * K
        WDST = [A.f32("WDST%d" % i, o + i * 4 * K, 1024, 3, 4) for i in range(2)]; o += 8 * K
        WD0 = A.bf("WD0", o, GMAX * D, 3, 4).rearrange("p (j n) -> p j n", j=GMAX); o += 12 * K
        SG = [A.f32("SG%d" % i, o + i * 2 * K, 512, 3, 4) for i in range(2)]; o += 4 * K
        assert o <= RWT, (o, RWT)
        WD1 = A.bf("WD1", RWT, GMAX * D, 4, 4).rearrange("p (j n) -> p j n", j=GMAX)
        WD = [WD0, WD1]

        S = Sched(nc)
        pbank = [0]

        busy_banks = set()

        def nb():
            while True:
                b = pbank[0]
                pbank[0] = (b + 1) % 8
                if b not in busy_banks:
                    return b

        def pe_warm(n):
            bw_ = nb()

            def warm_(e):
                ins = None
                for q in range(n):
                    ins = e.matmul(banks[bw_][:, 0:128], IDB[:, :], IDB[:, :], start=True, stop=True)
                return ins
            S.add("pe", warm_, reads=[("IDB",)], writes=[("ps", bw_)])

        def bankbf(b):
            return banks[b][:].bitcast(BF16).rearrange("p (k t) -> p k t", k=8)

        wa_v = w_ada.rearrange("(k p) n -> p k n", p=128)

        class AdaCtx:
            def __init__(self, was, ncols, modrow, badas, tag):
                self.was, self.ncols, self.modrow, self.badas, self.tag, self.cnt = was, ncols, modrow, badas, tag, 0
                self.bias_of = {}

        def ada_dma(ctx, s, p):
            sl = ctx.cnt % len(ctx.was)
            ctx.cnt += 1
            n = ctx.ncols
            c0 = s * 1024 + p * n
            S.add("pool", lambda e: e.dma_start(out=ctx.was[sl][:, :, :], in_=wa_v[:, :, c0:c0 + n]),
                  writes=[(ctx.tag + "WA", sl)], dma=(ctx.tag + "wa", sl))
            if ctx.tag == "2":
                ctx.bias_of[(s, p)] = None
                if p == 0:
                    S.add("sp", lambda e: e.dma_start(out=ctx.modrow[0:1, :], in_=b_ada[:, s * 1024:(s + 1) * 1024]),
                          writes=[(ctx.tag + "MODROW", q) for q in range(1024 // n)], dma=(ctx.tag + "bada", 0))
            else:
                bi = s % len(ctx.badas)
                ctx.bias_of[(s, p)] = (bi, p * n)
                if p == 0:
                    S.add("sp", lambda e: e.dma_start(out=ctx.badas[bi][0:1, :], in_=b_ada[:, s * 1024:(s + 1) * 1024]),
                          writes=[(ctx.tag + "BADA", bi)] + ([("GFROW",)] if bi == 1 else []),
                          dma=(ctx.tag + "bada", bi))
            return sl

        def m1_load(i):
            sl = i % NXIN
            S.add("sp", lambda e: e.dma_start(out=XIN[sl][:, :], in_=x_d[i * 128:(i + 1) * 128, :]),
                  writes=[("XIN", sl)], dma=("xin", sl))

        S.add("sp", lambda e: e.dma_start(out=VEC[:, 0:40], in_=vec_d[:, :]), writes=[("VEC",)], dma=("vec",))
        for i_ in range(NXIN):
            m1_load(i_)
        ctx1 = AdaCtx(WA, 256, MODROW, [BADA, BADA1], "")
        pre_sl = [ada_dma(ctx1, 0, p_) for p_ in range(4)]
        S.add("sp", lambda e: e.dma_start(out=GFROW[0:1, :], in_=gf_d[:, :]), writes=[("GFROW",)], dma=("gfrow",))
        S.add("sp", lambda e: e.dma_start(out=DWWT[:, 0:124], in_=dwwt_d[:, :]), writes=[("DWWT",)], dma=("dwwt",))
        win_v = w_in.rearrange("(k p) n -> p k n", p=128)

        def win_load(blk):
            S.add("pool", lambda e: e.dma_start(out=WIN[:, :, blk * 512:(blk + 1) * 512], in_=win_v[:, :, blk * 512:(blk + 1) * 512]),
                  writes=[("WIN", blk)], dma=("win", blk))
        S.add("pool", lambda e: e.memset(IDF, 0.0), writes=[("IDF",)])
        S.add("pool", lambda e: e.affine_select(out=IDF, in_=IDF, compare_op=ALU.not_equal, fill=1.0, base=0,
                                                 pattern=[[-1, 128]], channel_multiplier=1),
              reads=[("IDF",)], writes=[("IDF",)])
        S.add("pool", lambda e: e.tensor_copy(IDB, IDF), reads=[("IDF",)], writes=[("IDB",)])
        S.add("pool", lambda e: e.memset(ONES, 1.0), writes=[("ONES",)])
        S.add("pool", lambda e: e.memset(NEGH, -0.5), writes=[("NEGH",)])
        S.add("pool", lambda e: e.memset(EPSC, EPS), writes=[("EPSC",)])
        S.add("pool", lambda e: e.memset(SS, 0.0), writes=[("SS", w_, i_) for w_ in range(3) for i_ in range(NT)])
        S.add("pool", lambda e: e.memset(GLU[:, :, 0:30], 0.0), writes=[("GLUPAD",)])
        S.add("pool", lambda e: e.memset(VV[1][:, :, 512:528], 0.0), writes=[("V", 1, "tail")])

        def scratch_init(e):
            ins = None
            for t_ in T12 + RBUF:
                ins = e.memset(t_[:, 0:16], 0.0)
            return ins
        S.add("pool", scratch_init, writes=[("T", 0), ("T", 1)] + [("R", q) for q in range(4)])

        def cnt_fill(e):
            ins = None
            for g in range(4):
                w = 2 << g
                for t in range(16):
                    ins = e.memset(RCNT3[:, g, t:t + 1], 1.0 / min(t + 1, w))
            return ins
        S.add("pool", cnt_fill, writes=[("RCNT",)])

        S.add("act", lambda e: e.activation(CACTF[:, 0:8], VEC[:, 0:8], AF.Silu), reads=[("VEC",)], writes=[("CACTF",)])
        S.add("dve", lambda e: e.tensor_copy(CACT[:, 0:8], CACTF[:, 0:8]), reads=[("CACTF",)], writes=[("CACT",)])
        S.add("act", lambda e: e.activation(FJ[:, 24:25], EPSC[:, 0:1], AF.Sigmoid), reads=[("EPSC",)], writes=[("FJ", 24)])

        def ada_mm(ctx, s, p, sl):
            n = ctx.ncols
            b = nb()

            def mm(e):
                ins = None
                for k in range(8):
                    ins = e.matmul(banks[b][0:1, 0:n], CACT[:, k:k + 1], ctx.was[sl][:, k, :], start=(k == 0), stop=(k == 7))
                return ins
            S.add("pe", mm, reads=[("CACT",), (ctx.tag + "WA", sl)], writes=[("ps", b)])
            if ctx.bias_of[(s, p)] is None:
                S.add("dve", lambda e: e.tensor_tensor(ctx.modrow[0:1, p * n:(p + 1) * n], banks[b][0:1, 0:n],
                                                         ctx.modrow[0:1, p * n:(p + 1) * n], ALU.add),
                      reads=[("ps", b), (ctx.tag + "MODROW", p)], writes=[(ctx.tag + "MODROW", p)])
                return
            bi, boff = ctx.bias_of[(s, p)]
            S.add("dve", lambda e: e.tensor_tensor(ctx.modrow[0:1, p * n:(p + 1) * n], banks[b][0:1, 0:n],
                                                     ctx.badas[bi][0:1, boff:boff + n], ALU.add),
                  reads=[("ps", b), (ctx.tag + "BADA", bi)], writes=[(ctx.tag + "MODROW", p)])

        def row_to_cols(ctx):
            b = nb()
            MODROW = ctx.modrow

            def mm(e):
                ins = None
                for c in range(8):
                    ins = e.matmul(banks[b][:, c:c + 1], MODROW[0:1, c * 128:(c + 1) * 128], ONES[0:1, 0:1],
                                   start=True, stop=True)
                return ins
            S.add("pe", mm, reads=[(ctx.tag + "MODROW", p) for p in range(1024 // ctx.ncols)] + [("ONES",)], writes=[("ps", b)])
            return b

        def row_bcast(row, rkeys, dst, dkey):
            for hf in range(2):
                b = nb()
                S.add("pe", (lambda b, hf: lambda e: e.matmul(banks[b][:, :], ONES[0:1, :], row[0:1, hf * 512:(hf + 1) * 512],
                                                               start=True, stop=True))(b, hf),
                      reads=list(rkeys) + [("ONES",)], writes=[("ps", b)])
                S.add("dve", (lambda b, hf: lambda e: e.tensor_copy(dst[:, hf * 512:(hf + 1) * 512], banks[b][:, :]))(b, hf),
                      reads=[("ps", b)], writes=[(dkey, hf)])

        def ada_finish(ctx, s):
            MK = [(ctx.tag + "MODROW", p) for p in range(1024 // ctx.ncols)]
            if s == 0:
                b = row_to_cols(ctx)
                S.add("dve", lambda e: e.tensor_copy(AB[:, 8:16], banks[b][:, 0:8]), reads=[("ps", b)], writes=[("AB", 1)])
            elif s == 1:
                b = row_to_cols(ctx)
                S.add("dve", lambda e: e.scalar_tensor_tensor(AB[:, 0:8], banks[b][:, 0:8], 1.0, VEC[:, 8:16], ALU.add, ALU.mult),
                      reads=[("ps", b), ("VEC",)], writes=[("AB", 0)])
            elif s == 2:
                row_bcast(ctx.modrow, MK, GT1B, "GT1B")
            elif s == 3:
                b = row_to_cols(ctx)
                S.add("dve", lambda e: e.tensor_copy(AB[:, 24:32], banks[b][:, 0:8]), reads=[("ps", b)], writes=[("AB", 3)])
            elif s == 4:
                b = row_to_cols(ctx)
                S.add("dve", lambda e: e.scalar_tensor_tensor(AB[:, 16:24], banks[b][:, 0:8], 1.0, VEC[:, 16:24], ALU.add, ALU.mult),
                      reads=[("ps", b), ("VEC",)], writes=[("AB", 2)])
            else:
                row_bcast(ctx.modrow, MK, GT2B, "GT2B")

        row_bcast(GFROW, [("GFROW",)], GFB, "GFB")
        for p_ in range(4):
            ada_mm(ctx1, 0, p_, pre_sl[p_])
        ada_finish(ctx1, 0)
        sl1 = [ada_dma(ctx1, 1, p_) for p_ in range(4)]
        for blk in (1, 0, 2):
            win_load(blk)
        for p_ in range(4):
            ada_mm(ctx1, 1, p_, sl1[p_])
        ada_finish(ctx1, 1)

        def diag_build(i0, i1):
            def f(e):
                ins = None
                for idx in range(i0, i1):
                    if idx % 31 < 6:
                        continue
                    ins = e.tensor_tensor(DIAG[:, idx, :], IDF[:, :], DWWT[:, idx:idx + 1].to_broadcast([128, 128]), ALU.mult)
                return ins
            S.add("pool", f, reads=[("IDF",), ("DWWT",)], writes=[("DIAGP", i0 // 8)])
        DIAGK = [("DIAGP", q) for q in range(16)]

        def normA(i, which, src_ap, src_keys, hdst, hkey, defer_b=False):
            S.add("act", lambda e: e.activation(JUNK[:, :], src_ap, AF.Square, accum_out=SS3[:, which, i:i + 1]),
                  reads=src_keys, writes=[("JUNK",), ("SS", which, i)])
            S.add("dve", lambda e: e.tensor_scalar(RS3[:, which, i:i + 1], SS3[:, which, i:i + 1], 1.0 / D, EPS, ALU.mult, ALU.add),
                  reads=[("SS", which, i)], writes=[("RS", which, i)])
            S.add("pool", lambda e: e.tensor_tensor(RS3[:, which, i:i + 1], RS3[:, which, i:i + 1], NEGH[:, 0:1], ALU.pow),
                  reads=[("RS", which, i), ("NEGH",)], writes=[("RS", which, i)])
            def part_b():
                S.add("dve", lambda e: e.tensor_scalar(hdst, src_ap, RS3[:, which, i:i + 1], None, ALU.mult),
                      reads=src_keys + [("RS", which, i)], writes=[hkey])
            if hdst is None:
                return None
            if defer_b:
                return part_b
            part_b()
            return None

        def transposeT(hsrc, hkey, abcol, dst_fn, dkey, act_only=False):
            if act_only:
                b0_ = nb()
                bv0 = bankbf(b0_)

                def tr0(e):
                    ins = None
                    for k in range(8):
                        ins = e.transpose(bv0[:, k, :], hsrc[:, k * 128:(k + 1) * 128], IDB[:, :])
                    return ins
                S.add("pe", tr0, reads=[hkey, ("IDB",)], writes=[("ps", b0_)])

                def ev0(e):
                    ins = None
                    for k in range(8):
                        ins = e.activation(dst_fn(k), bv0[:, k, :], AF.Identity, bias=AB[:, abcol + 8 + k:abcol + 9 + k],
                                           scale=AB[:, abcol + k:abcol + k + 1])
                    return ins
                S.add("act", ev0, reads=[("ps", b0_), ("AB", abcol // 8), ("AB", abcol // 8 + 1)], writes=[dkey + ("a",), dkey + ("b",)])
                return
            b1_ = nb()
            b2_ = nb()
            bv1 = bankbf(b1_)
            bv2 = bankbf(b2_)

            def tr1(e):
                ins = None
                for k in range(0, 5):
                    ins = e.transpose(bv1[:, k, :], hsrc[:, k * 128:(k + 1) * 128], IDB[:, :])
                return ins

            def tr2(e):
                ins = None
                for k in range(5, 8):
                    ins = e.transpose(bv2[:, k, :], hsrc[:, k * 128:(k + 1) * 128], IDB[:, :])
                return ins
            S.add("pe", tr1, reads=[hkey, ("IDB",)], writes=[("ps", b1_)])
            S.add("pe", tr2, reads=[hkey, ("IDB",)], writes=[("ps", b2_)])

            def ev(e):
                ins = None
                for k in range(0, 5):
                    ins = e.activation(dst_fn(k), bv1[:, k, :], AF.Identity, bias=AB[:, abcol + 8 + k:abcol + 9 + k],
                                       scale=AB[:, abcol + k:abcol + k + 1])
                return ins

            def ev2(e):
                ins = None
                for k in range(5, 8):
                    ins = e.tensor_scalar(dst_fn(k), bv2[:, k, :], AB[:, abcol + k:abcol + k + 1], AB[:, abcol + 8 + k:abcol + 9 + k],
                                          ALU.mult, ALU.add)
                return ins
            S.add("act", ev, reads=[("ps", b1_), ("AB", abcol // 8), ("AB", abcol // 8 + 1)], writes=[dkey + ("a",)])
            S.add("dve", ev2, reads=[("ps", b2_), ("AB", abcol // 8), ("AB", abcol // 8 + 1)], writes=[dkey + ("b",)])

        def m1_normA(i):
            sl = i % 2
            xs = i % NXIN
            return normA(i, 0, XIN[xs][:, :], [("XIN", xs)], HB[sl][:, :], ("H", sl), defer_b=True)

        def m1_T(i):
            sl = i % 2
            st_, q = divmod(i, 4)
            hs = st_ % 2
            transposeT(HB[sl], ("H", sl), 0, lambda k: HT[hs][:, k, q * 128:(q + 1) * 128], ("HT", hs, q), act_only=(i >= 4))

        def p2_chunk(st_, kind, cc):
            hs = st_ % 2
            m = {"g": 4 + cc, "a": cc, "p": 8 + cc}[kind]
            b = nb()

            def mm(e):
                ins = None
                for k in range(8):
                    ins = e.matmul(banks[b][:, :], WIN[:, k, m * 128:(m + 1) * 128], HT[hs][:, k, :],
                                   start=(k == 0), stop=(k == 7))
                return ins
            S.add("pe", mm, reads=[("WIN", m // 4)] + [("HT", hs, q, h_) for q in range(4) for h_ in "ab"], writes=[("ps", b)])
            if kind == "g":
                sl = 0
                S.add("act", lambda e: e.activation(SIG[sl][:, :], banks[b][:, :], AF.Sigmoid),
                      reads=[("ps", b)], writes=[("SIG", sl)])
            elif kind == "a":
                sl = 0
                S.add("dve", lambda e: e.tensor_tensor(GLU[:, cc, 30 + st_ * 512:30 + (st_ + 1) * 512], banks[b][:, :], SIG[sl][:, :], ALU.mult),
                      reads=[("ps", b), ("SIG", sl)], writes=[("GLU", cc, st_)])
            else:
                vs = st_ % 2
                S.add("act", lambda e: e.copy(VV[vs][:, cc, 16:528], banks[b][:, :]),
                      reads=[("ps", b)], writes=[("V", vs, cc)])

        def pool_branch(st_, g):
            vs = st_ % 2
            V = VV[vs]
            Vp = VV[1 - vs]
            S.add("dve", lambda e: e.tensor_copy(V[:, g, 0:16], Vp[:, g, 512:528]),
                  reads=[("V", 1 - vs, g), ("V", 1 - vs, "tail")], writes=[("V", vs, g, "halo")])
            rk = [("V", vs, g), ("V", vs, g, "halo")]
            rb = g
            chain = [(V[:, g, :], None)]
            for step in range(g):
                chain.append((T12[step % 2][:, :], ("T", step % 2)))
            chain.append((RBUF[rb][:, :], ("R", rb)))
            sh = 1
            for step in range(g + 1):
                a_src, ak = chain[step]
                d, dk = chain[step + 1]
                S.add("dve", (lambda d, a_src, sh: lambda e: e.tensor_tensor(d[:, sh:528], a_src[:, sh:528], a_src[:, 0:528 - sh], ALU.add))(d, a_src, sh),
                      reads=rk + ([ak] if ak else []), writes=[dk])
                sh *= 2
            w = 2 << g
            sw = RBUF[rb]
            tk = ("R", rb)

            def fin():
                S.add("dve", lambda e: e.scalar_tensor_tensor(PP[:, g, st_ * 512:(st_ + 1) * 512], sw[:, 16:528], 1.0 / w, V[:, g, 16:528],
                                                              ALU.mult, ALU.subtract),
                      reads=[tk] + rk, writes=[("PP", g, st_)])
                if st_ == 0:
                    S.add("dve", lambda e: e.tensor_tensor(sw[:, 16:32], sw[:, 16:32], RCNT3[:, g, :], ALU.mult),
                          reads=[tk, ("RCNT",)], writes=[tk])
                    S.add("dve", lambda e: e.tensor_tensor(PP[:, g, 0:16], sw[:, 16:32], V[:, g, 16:32], ALU.subtract),
                          reads=[tk] + rk, writes=[("PP", g, st_)])
            return fin

        m1_normA(0)()
        pend = []
        late = []
        cur_i = [0]
        for i in range(NT):
            cur_i[0] = i
            nb_ = m1_normA(i + 1) if i + 1 < NT else None
            m1_T(i)
            if nb_:
                nb_()
            if i + NXIN < NT:
                m1_load(i + NXIN)
            for _ in range(3):
                if pend:
                    pend.pop(0)()
            while late and late[0][0] <= i:
                late.pop(0)[1]()
            if i % 4 == 3:
                st_ = i // 4
                for cc in range(4):
                    pend.append((lambda st_, cc: lambda: p2_chunk(st_, "g", cc))(st_, cc))
                    pend.append((lambda st_, cc: lambda: p2_chunk(st_, "a", cc))(st_, cc))
                for g in range(4):
                    pend.append((lambda st_, g: lambda: (p2_chunk(st_, "p", g), late.append((cur_i[0], pool_branch(st_, g)))))(st_, g))
                if st_ == 0:
                    bw = nb()

                    def warm(e):
                        ins = None
                        for q in range(28):
                            ins = e.matmul(banks[bw][:, 0:128], IDB[:, :], IDB[:, :], start=True, stop=True)
                        return ins
                    S.add("pe", warm, reads=[("IDB",)], writes=[("ps", bw)])
                for _ in range(2):
                    pend.pop(0)()
            if i * 8 < 124:
                diag_build(i * 8, min(124, i * 8 + 8))
            if i == 8:
                wpw_v = w_pw.rearrange("(k p) n -> p k n", p=128)
                S.add("pool", lambda e: e.dma_start(out=WPW[:, :, :], in_=wpw_v[:, :, :]), writes=[("WPW",)], dma=("wpw",))
                wpg_v = w_pg.rearrange("g c d -> c g d")
                S.add("pool", lambda e: e.dma_start(out=WPG[:, :, :], in_=wpg_v[:, :, :]), writes=[("WPG",)], dma=("wpg",))
        i = NT
        while pend or late:
            cur_i[0] = i
            for _ in range(3):
                if pend:
                    pend.pop(0)()
            while late and (late[0][0] <= i or not pend):
                late.pop(0)[1]()
            i += 1

        S.fence("pool", lambda e: e.memset(FJ[:, 0:2], 0.0),
                ["WIN", "HT", "XIN", "H", "SIG", "T"],
                ["CONV", "SQ", "MUB", "RSB", "TMB", "LT", "ZT", "2WA", "2MODROW", "2BADA", "XPRE"])
        S.fence("pool", lambda e: e.memset(FJ[:, 2:4], 0.0), ["V", "WA"], ["YT"])
        M4ORD = list(range(NT))
        S.add("sp", lambda e: e.dma_start(out=X[:, NT - 1, :], in_=x_d[(NT - 1) * 128:NT * 128, :]),
              writes=[("X", NT - 1, 0), ("X", NT - 1, 1), ("XPRE",)], dma=("xld", 0))
        S.fence("pool", lambda e: e.memset(FJ[:, 4:6], 0.0), ["MODROW", "BADA", "GFROW", "R"], ["WOUT", "WOST"])

        def wout_prep(k, hf):
            sl = (k * 2 + hf) % 2
            S.add("sp", lambda e: e.dma_start(out=WOST[sl][:, :], in_=w_out[k * 128:(k + 1) * 128, hf * 512:(hf + 1) * 512]),
                  writes=[("WOST", sl)], dma=("wost", sl))
            S.add("pool", lambda e: e.tensor_tensor(WOUT[:, k, hf * 512:(hf + 1) * 512], WOST[sl][:, :], GT1B[:, hf * 512:(hf + 1) * 512], ALU.mult),
                  reads=[("WOST", sl), ("GT1B", hf)], writes=[("WOUT", k, hf)])

        wprep = [(k, hf) for k in range(8) for hf in range(2)]

        NDT = 6

        def m3_taps(st_):
            cs = st_ % 2
            for j in range(NDT):
                for cc in range(4):
                    gk = [("GLUPAD",), ("GLU", cc, st_)] + ([("GLU", cc, st_ - 1)] if st_ > 0 else [])
                    if j == 0:
                        S.add("dve", (lambda cc: lambda e: e.tensor_scalar(
                            CONV[cs][:, cc, :], GLU[:, cc, st_ * 512:st_ * 512 + 512], DWWT[:, cc * 31:cc * 31 + 1],
                            VEC[:, 24 + cc:25 + cc], ALU.mult, ALU.add))(cc),
                            reads=gk + [("DWWT",), ("VEC",)], writes=[("CONV", cs, cc)])
                    else:
                        S.add("dve", (lambda cc, j: lambda e: e.scalar_tensor_tensor(
                            CONV[cs][:, cc, :], GLU[:, cc, st_ * 512 + j:st_ * 512 + j + 512],
                            DWWT[:, cc * 31 + j:cc * 31 + j + 1], CONV[cs][:, cc, :], ALU.mult, ALU.add))(cc, j),
                            reads=gk + [("DWWT",), ("CONV", cs, cc)], writes=[("CONV", cs, cc)])

        def m3_conv(st_, cc):
            cs = st_ % 2
            b = nb()
            gk = [("GLUPAD",), ("GLU", cc, st_)] + ([("GLU", cc, st_ - 1)] if st_ > 0 else [])

            def mm(e):
                ins = None
                for j in range(NDT, KW):
                    ins = e.matmul(banks[b][:, :], DIAG[:, cc * 31 + j, :], GLU[:, cc, st_ * 512 + j:st_ * 512 + j + 512],
                                   start=(j == NDT), stop=(j == KW - 1))
                return ins
            S.add("pe", mm, reads=DIAGK + gk, writes=[("ps", b)])
            S.add("dve", lambda e: e.tensor_tensor(CONV[cs][:, cc, :], CONV[cs][:, cc, :], banks[b][:, :], ALU.add),
                  reads=[("ps", b), ("CONV", cs, cc)], writes=[("CONV", cs, cc)])
            S.add("act", lambda e: e.activation(SQ[:, cc, :], CONV[cs][:, cc, :], AF.Square),
                  reads=[("CONV", cs, cc)], writes=[("SQ", cc)])

        def m3_ln(st_):
            cs = st_ % 2
            zs = st_ % 2
            b1_ = nb()
            b2_ = nb()

            def mm1(e):
                ins = None
                for cc in range(4):
                    ins = e.matmul(banks[b1_][:, :], ONES[:, :], CONV[cs][:, cc, :], start=(cc == 0), stop=(cc == 3))
                return ins

            def mm2(e):
                ins = None
                for cc in range(4):
                    ins = e.matmul(banks[b2_][:, :], ONES[:, :], SQ[:, cc, :], start=(cc == 0), stop=(cc == 3))
                return ins
            S.add("pe", mm1, reads=[("CONV", cs, cc) for cc in range(4)] + [("ONES",)], writes=[("ps", b1_)])
            S.add("pe", mm2, reads=[("SQ", cc) for cc in range(4)] + [("ONES",)], writes=[("ps", b2_)])
            S.add("dve", lambda e: e.tensor_scalar(MUB[:, :], banks[b1_][:, :], 1.0 / CW, None, ALU.mult),
                  reads=[("ps", b1_)], writes=[("MUB",)])
            S.add("dve", lambda e: e.tensor_tensor(TMB[:, :], MUB[:, :], MUB[:, :], ALU.mult), reads=[("MUB",)], writes=[("TMB",)])
            S.add("dve", lambda e: e.scalar_tensor_tensor(RSB[:, :], banks[b2_][:, :], 1.0 / CW, TMB[:, :], ALU.mult, ALU.subtract),
                  reads=[("ps", b2_), ("TMB",)], writes=[("RSB",)])
            S.add("dve", lambda e: e.tensor_scalar(RSB[:, :], RSB[:, :], 0.0, None, ALU.max), reads=[("RSB",)], writes=[("RSB",)])
            S.add("act", lambda e: e.activation(TMB[:, :], RSB[:, :], AF.Ln, bias=EPSC[:, 0:1]),
                  reads=[("RSB",), ("EPSC",)], writes=[("TMB",)])
            S.add("act", lambda e: e.activation(RSB[:, :], TMB[:, :], AF.Exp, scale=-0.5),
                  reads=[("TMB",)], writes=[("RSB",)])
            for cc in range(4):
                ls = cc % 2
                S.add("dve", (lambda cc, ls: lambda e: e.tensor_tensor(LT[ls][:, :], CONV[cs][:, cc, :], MUB[:, :], ALU.subtract))(cc, ls),
                      reads=[("CONV", cs, cc), ("MUB",)], writes=[("LT", ls)])
                S.add("dve", (lambda cc, ls: lambda e: e.tensor_tensor(LT[ls][:, :], LT[ls][:, :], RSB[:, :], ALU.mult))(cc, ls),
                      reads=[("LT", ls), ("RSB",)], writes=[("LT", ls)])
                S.add("act", (lambda cc, ls: lambda e: e.activation(ZT[zs][:, cc, :], LT[ls][:, :], AF.Silu,
                                                                     bias=VEC[:, 32 + cc:33 + cc], scale=VEC[:, 28 + cc:29 + cc]))(cc, ls),
                      reads=[("LT", ls), ("VEC",)], writes=[("ZT", zs, cc)])

        def m3_pw(st_, m):
            zs = st_ % 2
            b = nb()

            def mm(e):
                ins = None
                for cc in range(4):
                    ins = e.matmul(banks[b][:, :], WPW[:, cc, m * 128:(m + 1) * 128], ZT[zs][:, cc, :], start=(cc == 0), stop=(cc == 3))
                return ins
            S.add("pe", mm, reads=[("WPW",)] + [("ZT", zs, cc) for cc in range(4)], writes=[("ps", b)])
            S.add("act", lambda e: e.copy(YT[:, m, st_ * 512:(st_ + 1) * 512], banks[b][:, :]),
                  reads=[("ps", b)], writes=[("YT", m, st_)])

        def m3_pg(st_, g):
            b = nb()
            S.add("pe", lambda e: e.matmul(banks[b][:, :], WPG[:, g, :], PP[:, g, st_ * 512:(st_ + 1) * 512], start=True, stop=True),
                  reads=[("WPG",), ("PP", g, st_)], writes=[("ps", b)])
            S.add("act", lambda e: e.activation(YT[:, 4 + g, st_ * 512:(st_ + 1) * 512], banks[b][:, :], AF.Identity, scale=VEC[:, 36 + g:37 + g]),
                  reads=[("ps", b), ("VEC",)], writes=[("YT", 4 + g, st_)])

        def m4_out(i, defer_adds=False):
            bs = [nb(), nb()]

            def mm(e):
                ins = None
                for k in range(8):
                    for hf in range(2):
                        ins = e.matmul(banks[bs[hf]][:, :], YT[:, k, i * 128:(i + 1) * 128], WOUT[:, k, hf * 512:(hf + 1) * 512],
                                       start=(k == 0), stop=(k == 7))
                return ins
            S.add("pe", mm, reads=[("YT", k, i // 4) for k in range(8)] + [("WOUT", k, hf) for k in range(8) for hf in range(2)],
                  writes=[("ps", bs[0]), ("ps", bs[1])])
            def adds():
                busy_banks.difference_update(bs)
                for hf in range(2):
                    S.add("dve", (lambda hf: lambda e: e.tensor_tensor(X[:, i, hf * 512:(hf + 1) * 512], X[:, i, hf * 512:(hf + 1) * 512],
                                                                       banks[bs[hf]][:, :], ALU.add))(hf),
                          reads=[("ps", bs[hf]), ("X", i, hf)], writes=[("X", i, hf)])
            if defer_adds:
                busy_banks.update(bs)
                return adds
            adds()
            return None

        ctx2 = AdaCtx(WA2, 256, MODROW2, [], "2")
        m3_taps(0)
        for cc in range(4):
            m3_conv(0, cc)
        pe_warm(40)
        next_sls = None
        for st_ in range(NST):
            s_ = 2 + st_
            sls = next_sls if next_sls is not None else [ada_dma(ctx2, s_, p_) for p_ in range(3)]
            next_sls = None
            if st_ + 1 < NST:
                m3_taps(st_ + 1)
            m3_ln(st_)
            if st_ + 1 < NST:
                for cc in range(4):
                    m3_conv(st_ + 1, cc)
                    ada_mm(ctx2, s_, cc, sls[cc])
                    if cc == 0:
                        sls.append(ada_dma(ctx2, s_, 3))
                ada_finish(ctx2, s_)
                next_sls = [ada_dma(ctx2, s_ + 1, p_) for p_ in range(3)]
                for g in range(4):
                    m3_pg(st_, g)
            else:
                ada_mm(ctx2, s_, 0, sls[0])
                sls.append(ada_dma(ctx2, s_, 3))
                for g in range(4):
                    m3_pg(st_, g)
                ada_mm(ctx2, s_, 1, sls[1])
                ada_mm(ctx2, s_, 2, sls[2])
                early_bank0 = pbank[0]
                early_adds = [m4_out(0, defer_adds=True), m4_out(1, defer_adds=True)]
                ada_mm(ctx2, s_, 3, sls[3])
                ada_finish(ctx2, s_)
            for m in range(4):
                m3_pw(st_, m)
            for _ in range(6):
                if wprep:
                    wout_prep(*wprep.pop(0))
        while wprep:
            wout_prep(*wprep.pop(0))

        S.fence("pool", lambda e: e.memset(FJ[:, 6:8], 0.0),
                ["CONV", "SQ", "MUB", "RSB", "TMB", "LT", "ZT", "2WA", "2MODROW", "2BADA"], ["X"])
        S.fence("pool", lambda e: e.memset(FJ[:, 8:10], 0.0), ["GLU", "GLUPAD", "PP"], ["H2T"])
        S.fence("pool", lambda e: e.memset(FJ[:, 16:18], 0.0), ["GT1B"], ["H2B"])
        S.fence("pool", lambda e: e.memset(FJ[:, 10:12], 0.0), ["DIAGP", "WPW", "WPG"], ["WG", "WU", "WDST", "WD0", "SG"])

        wg_v = w_g.rearrange("(k p) n -> p k n", p=128)
        wu_v = w_u.rearrange("(k p) n -> p k n", p=128)
        gu_count = [0]

        def gu_load(j):
            ws = gu_count[0] % 3
            gu_count[0] += 1
            S.add("pool", lambda e: e.dma_start(out=WG[ws][:, :, :], in_=wg_v[:, :, j * 128:(j + 1) * 128]),
                  writes=[("WG", ws)], dma=("wg", ws))
            S.add("pool", lambda e: e.dma_start(out=WU[ws][:, :, :], in_=wu_v[:, :, j * 128:(j + 1) * 128]),
                  writes=[("WU", ws)], dma=("wu", ws))
            return ws

        wd_count = [0]

        def wd_prep(gi, jj):
            j = GROUPS[gi][jj]
            sl = wd_count[0] % 2
            wd_count[0] += 1
            S.add("sp", lambda e: e.dma_start(out=WDST[sl][:, :], in_=w_d[j * 128:(j + 1) * 128, :]),
                  writes=[("WDST", sl)], dma=("wdst", sl))
            S.add("dve", lambda e: e.tensor_tensor(WD[gi % 2][:, jj, :], WDST[sl][:, :], GT2B[:, :], ALU.mult),
                  reads=[("WDST", sl), ("GT2B", 0), ("GT2B", 1)], writes=[("WD%d" % (gi % 2), jj)])

        def m4_load(i, n):
            S.add("sp", lambda e: e.dma_start(out=X[:, i, :], in_=x_d[i * 128:(i + 1) * 128, :]),
                  writes=[("X", i, 0), ("X", i, 1)], dma=("xld", n % 6))

        def m4_norm(i, n):
            sl = n % 2
            return normA(i, 1, X[:, i, :], [("X", i, 0), ("X", i, 1)], H2B[sl][:, :], ("H2B", sl), defer_b=True)

        def m4_T(i, n):
            sl = n % 2
            transposeT(H2B[sl], ("H2B", sl), 16, lambda k: H2T[:, k, i * 128:(i + 1) * 128], ("H2T", i))

        def ffn_gu(gi, jj, j, ws, st_):
            bg = nb()
            bu = nb()

            def mmg(e):
                ins = None
                for k in range(8):
                    ins = e.matmul(banks[bg][:, :], WG[ws][:, k, :], H2T[:, k, st_ * 512:(st_ + 1) * 512], start=(k == 0), stop=(k == 7))
                return ins

            def mmu(e):
                ins = None
                for k in range(8):
                    ins = e.matmul(banks[bu][:, :], WU[ws][:, k, :], H2T[:, k, st_ * 512:(st_ + 1) * 512], start=(k == 0), stop=(k == 7))
                return ins
            hk = [("H2T", st_ * 4 + q, h_) for q in range(4) for h_ in "ab"]
            S.add("pe", mmg, reads=[("WG", ws)] + hk, writes=[("ps", bg)])
            S.add("pe", mmu, reads=[("WU", ws)] + hk, writes=[("ps", bu)])
            sl = st_ % 2
            S.add("act", lambda e: e.activation(SG[sl][:, :], banks[bg][:, :], AF.Silu), reads=[("ps", bg)], writes=[("SG", sl)])
            S.add("dve", lambda e: e.tensor_tensor(ACT_[:, jj, st_ * 512:(st_ + 1) * 512], banks[bu][:, :], SG[sl][:, :], ALU.mult),
                  reads=[("ps", bu), ("SG", sl)], writes=[("A", jj, st_)])

        done_gu = set()
        for n in range(0, 7):
            m4_load(M4ORD[n], n)
        for a_ in early_adds:
            a_()
        pbank[0] = (early_bank0 + 4) % 8
        pre = [(lambda jj: lambda: wd_prep(0, jj))(jj) for jj in range(len(GROUPS[0]))]
        gu_slots = {}
        m4_norm(M4ORD[0], 0)()
        for n in range(NT):
            if n + 2 < NT:
                m4_out(M4ORD[n + 2])
            if n == 0:
                pe_warm(56)
            nb_ = m4_norm(M4ORD[n + 1], n + 1) if n + 1 < NT else None
            if n == NT - 2:
                S.fence("pool", lambda e: e.memset(FJ[:, 12:14], 0.0), ["YT"], ["A"])
                for st_ in range(2):
                    ffn_gu(0, 0, GROUPS[0][0], gu_slots[GROUPS[0][0]], st_)
                    done_gu.add((0, 0, st_))
            if n == NT - 1:
                ffn_gu(0, 0, GROUPS[0][0], gu_slots[GROUPS[0][0]], 2)
                done_gu.add((0, 0, 2))
            m4_T(M4ORD[n], n)
            if nb_:
                nb_()
            if n + 7 < NT - 1:
                m4_load(M4ORD[n + 7], n + 7)
            if n >= 8 and pre:
                pre.pop(0)()
            if n == 10:
                gu_slots[GROUPS[0][0]] = gu_load(GROUPS[0][0])
            if n == 12:
                gu_slots[GROUPS[0][1]] = gu_load(GROUPS[0][1])
        while pre:
            pre.pop(0)()

        S.fence("pool", lambda e: e.memset(FJ[:, 14:16], 0.0), ["WOUT", "WOST"], ["WD1"])
        H2K = [("H2T", i) for i in range(NT)]
        all_j = [j for g in GROUPS for j in g]

        def ffn_down(gi, i):
            nj = len(GROUPS[gi])
            bs = [nb(), nb()]

            def mm(e):
                ins = None
                for jj in range(nj):
                    for hf in range(2):
                        ins = e.matmul(banks[bs[hf]][:, :], ACT_[:, jj, i * 128:(i + 1) * 128], WD[gi % 2][:, jj, hf * 512:(hf + 1) * 512],
                                       start=(jj == 0), stop=(jj == nj - 1))
                return ins
            S.add("pe", mm, reads=[("A", jj, i // 4) for jj in range(nj)] + [("WD%d" % (gi % 2), jj) for jj in range(nj)],
                  writes=[("ps", bs[0]), ("ps", bs[1])])
            for hf in range(2):
                S.add("dve", (lambda hf: lambda e: e.tensor_tensor(X[:, i, hf * 512:(hf + 1) * 512], X[:, i, hf * 512:(hf + 1) * 512],
                                                                   banks[bs[hf]][:, :], ALU.add))(hf),
                      reads=[("ps", bs[hf]), ("X", i, hf)], writes=[("X", i, hf)])

        outs = []

        def final_a(i):
            normA(i, 2, X[:, i, :], [("X", i, 0), ("X", i, 1)], None, None)

        def final_b(i):
            S.add("dve", lambda e: e.scalar_tensor_tensor(X[:, i, :], X[:, i, :], RS3[:, 2, i:i + 1], GFB[:, :], ALU.mult, ALU.mult),
                  reads=[("X", i, 0), ("X", i, 1), ("RS", 2, i), ("GFB", 0), ("GFB", 1)], writes=[("X", i, 0), ("X", i, 1)])
            outs.append(S.add("sp", lambda e: e.dma_start(out=out_d[i * 128:(i + 1) * 128, :], in_=X[:, i, :]),
                              reads=[("X", i, 0), ("X", i, 1)], dma=("out", i % 4)))

        pos = 0
        for gi, grp in enumerate(GROUPS):
            nxt = []
            if gi + 1 < len(GROUPS):
                nxt = [(lambda gi2, jj: lambda: wd_prep(gi2, jj))(gi + 1, jj) for jj in range(len(GROUPS[gi + 1]))]
            for jj, j in enumerate(grp):
                for ahead in (pos, pos + 1, pos + 2):
                    if ahead < len(all_j) and all_j[ahead] not in gu_slots:
                        gu_slots[all_j[ahead]] = gu_load(all_j[ahead])
                ws = gu_slots[j]
                for st_ in range(NST):
                    if (gi, jj, st_) not in done_gu:
                        ffn_gu(gi, jj, j, ws, st_)
                if nxt:
                    nxt.pop(0)()
                pos += 1
            while nxt:
                nxt.pop(0)()
            last = gi == len(GROUPS) - 1
            for i in range(NT):
                ffn_down(gi, i)
                if last:
                    if i >= 1:
                        final_a(i - 1)
                    if i >= 2:
                        final_b(i - 2)
            if last:
                final_a(NT - 1)
                final_b(NT - 2)
                final_b(NT - 1)

        info = S.emit(finish_ops=outs)
    return nc, info


_CACHE = {}


def kernel(**inputs):
    f = lambda a: np.ascontiguousarray(np.asarray(a, dtype=np.float32))
    x = f(inputs["x"])
    c = f(inputs["c"])
    if "nc" not in _CACHE:
        _CACHE["nc"] = build_program()[0]
    nc = _CACHE["nc"]
    T_ = lambda a, n: f(a).reshape(n, 128).T
    dw = f(inputs["dw_w"])[0]
    dwwt = np.ascontiguousarray(dw.reshape(KW, 4, 128).transpose(2, 1, 0).reshape(128, 4 * KW))
    common = [T_(inputs["g_norm1"], 8), T_(inputs["g_norm2"], 8), T_(inputs["dw_b"], 4), T_(inputs["conv_ln_g"], 4),
              T_(inputs["conv_ln_b"], 4), T_(inputs["pool_scale"], 4)]
    shared = {
        "dwwt": dwwt,
        "w_ada": f(inputs["w_ada"])[0],
        "b_ada": f(inputs["b_ada"]).reshape(1, 6 * D),
        "w_in": f(inputs["w_in"])[0],
        "w_conv_pw": f(inputs["w_conv_pw"])[0],
        "w_pool_group": f(inputs["w_pool_group"])[0],
        "w_out": f(inputs["w_out"])[0],
        "w_ffn_gate": f(inputs["w_ffn_gate"])[0],
        "w_ffn_up": f(inputs["w_ffn_up"])[0],
        "w_ffn_down": f(inputs["w_ffn_down"])[0],
        "g_final": f(inputs["g_final"]).reshape(1, D),
    }
    in_maps = []
    for b in range(8):
        m = dict(shared)
        m["x"] = x[b]
        m["vecs"] = np.ascontiguousarray(np.concatenate([c[b].reshape(8, 128).T] + common, axis=1))
        in_maps.append(m)
    res = run_bass_kernel_spmd(nc, in_maps, core_ids=list(range(8)))
    return np.stack([np.asarray(r["out"], dtype=np.float32) for r in res.results], axis=0)
```

```python
import contextlib
import numpy as np
import concourse.bass as bass
import concourse.mybir as mybir
from concourse.bass_utils import run_bass_kernel_spmd

F32 = mybir.dt.float32
BF16 = mybir.dt.bfloat16
U8 = mybir.dt.uint8
AF = mybir.ActivationFunctionType
ALU = mybir.AluOpType
AX = mybir.AxisListType

D = 1024
SEQ = 2048
NT = 16
NST = 4
CW = 512
KW = 31
DFF = 2816
NJ = 22
EPS = 1e-6
GROUPS = [[0, 1, 2, 3, 4, 5], [6, 7, 8, 9, 10, 11], [12, 13, 14, 15, 16], [17, 18, 19, 20, 21]]
GMAX = 6
SYNC_SAME_ENGINE_WAW = True


class Op:
    __slots__ = ("eng", "fn", "is_dma", "deps", "idx", "ev", "need_ev", "semkey")


class Sched:
    ENGS = ("pe", "act", "dve", "pool", "sp")

    def __init__(self, nc):
        self.nc = nc
        self.ops = []
        self.last_w = {}
        self.readers = {}
        self.last_dma_by_key = {}
        self.buf_fence = {}

    def add(self, eng, fn, reads=(), writes=(), dma=None):
        op = Op()
        op.eng = eng
        op.fn = fn
        op.is_dma = dma is not None
        op.semkey = dma
        op.idx = len(self.ops)
        op.ev = None
        op.need_ev = False
        deps = {}

        def dep(o, raw):
            if o is not op:
                deps[o] = deps.get(o, False) or raw

        def lastw(k):
            w = self.last_w.get(k)
            if w is None:
                w = self.buf_fence.get(k[0])
            return w

        for r in reads:
            w = lastw(r)
            if w is not None:
                dep(w, True)
        for w_ in writes:
            w = lastw(w_)
            if w is not None:
                dep(w, SYNC_SAME_ENGINE_WAW)
            rd = self.readers.get(w_)
            if rd:
                for o in rd.values():
                    dep(o, SYNC_SAME_ENGINE_WAW)
        if op.is_dma:
            prev = self.last_dma_by_key.get(dma)
            if prev is not None:
                dep(prev, True)
            self.last_dma_by_key[dma] = op
        for w_ in writes:
            self.last_w[w_] = op
            self.readers[w_] = {}
        for r in reads:
            d = self.readers.setdefault(r, {})
            if op.is_dma:
                d[("dma", op.idx)] = op
            else:
                d[eng] = op
        final = []
        for o, raw in deps.items():
            if (not o.is_dma) and (not op.is_dma) and o.eng == eng:
                if eng == "pe" or not raw:
                    continue
            final.append(o)
            o.need_ev = True
        op.deps = final
        self.ops.append(op)
        return op

    def fence(self, eng, fn, old_bufs, new_bufs):
        olds = set(old_bufs)
        keys = [k for k in set(self.last_w) | set(self.readers) if k[0] in olds]
        op = self.add(eng, fn, reads=(), writes=keys)
        for b in new_bufs:
            self.buf_fence[b] = op
        return op

    def emit(self, finish_ops=()):
        nc = self.nc
        with contextlib.ExitStack() as st:
            eng_sem = {e: st.enter_context(nc.semaphore("s_" + e)) for e in self.ENGS}
            dma_keys = []
            seen = set()
            for op in self.ops:
                if op.is_dma and op.semkey not in seen:
                    seen.add(op.semkey)
                    dma_keys.append(op.semkey)
            dma_sem = {k: st.enter_context(nc.semaphore("d%d" % i)) for i, k in enumerate(dma_keys)}
            eng_cnt = {e: 0 for e in self.ENGS}
            dma_cnt = {k: 0 for k in dma_keys}
            per_eng = {e: [] for e in self.ENGS}
            for op in self.ops:
                per_eng[op.eng].append(op)
                if op.is_dma:
                    dma_cnt[op.semkey] += 16
                    op.ev = (dma_sem[op.semkey], dma_cnt[op.semkey], 16)
                elif op.need_ev:
                    eng_cnt[op.eng] += 1
                    op.ev = (eng_sem[op.eng], eng_cnt[op.eng], 1)
            fin_waits = [o.ev for o in finish_ops]
            block = st.enter_context(nc.Block())

            def run(e_handle, ename):
                known = {}
                for op in per_eng[ename]:
                    for d in op.deps:
                        sem, val, _ = d.ev
                        k = id(sem)
                        if known.get(k, 0) >= val:
                            continue
                        e_handle.wait_ge(sem, val)
                        known[k] = val
                    ins = op.fn(e_handle)
                    if op.ev is not None:
                        ins.then_inc(op.ev[0], op.ev[2])
                if ename == "sp":
                    for sem, val, _ in fin_waits:
                        if known.get(id(sem), 0) >= val:
                            continue
                        e_handle.wait_ge(sem, val)
                        known[id(sem)] = val

            @block.tensor
            def _(e):
                run(e, "pe")

            @block.scalar
            def _(e):
                run(e, "act")

            @block.vector
            def _(e):
                run(e, "dve")

            @block.gpsimd
            def _(e):
                run(e, "pool")

            @block.sync
            def _(e):
                run(e, "sp")
        return {"n_ops": len(self.ops), "eng_cnt": eng_cnt, "n_dma_sems": len(dma_keys)}


class Arena:
    def __init__(self, ar, limit):
        self.ar = ar
        self.limit = limit
        self.items = []

    def raw(self, name, off, nbytes, p0, p1):
        assert off % 32 == 0, (name, off)
        assert off + nbytes <= self.limit, (name, off, nbytes, self.limit)
        for (n2, o2, b2, q0, q1) in self.items:
            if not (p1 < q0 or q1 < p0):
                assert off + nbytes <= o2 or o2 + b2 <= off, ("overlap", name, n2)
        self.items.append((name, off, nbytes, p0, p1))
        return self.ar[:, off:off + nbytes]

    def f32(self, name, off, n, p0, p1):
        return self.raw(name, off, n * 4, p0, p1).bitcast(F32)

    def bf(self, name, off, n, p0, p1):
        return self.raw(name, off, n * 2, p0, p1).bitcast(BF16)


def build_program():
    nc = bass.Bass("TRN2", target_bir_lowering=False)
    dt_in = lambda name, shape: nc.dram_tensor(name, list(shape), F32, kind="ExternalInput").ap()
    x_d = dt_in("x", [SEQ, D])
    vec_d = dt_in("vecs", [128, 40])
    dwwt_d = dt_in("dwwt", [128, 124])
    w_ada = dt_in("w_ada", [D, 6 * D])
    b_ada = dt_in("b_ada", [1, 6 * D])
    w_in = dt_in("w_in", [D, 3 * CW])
    w_pw = dt_in("w_conv_pw", [CW, CW])
    w_pg = dt_in("w_pool_group", [4, 128, 128])
    w_out = dt_in("w_out", [D, D])
    w_g = dt_in("w_ffn_gate", [D, DFF])
    w_u = dt_in("w_ffn_up", [D, DFF])
    w_d = dt_in("w_ffn_down", [DFF, D])
    gf_d = dt_in("g_final", [1, D])
    out_d = nc.dram_tensor("out", [SEQ, D], F32, kind="ExternalOutput").ap()

    K = 1024
    LIMIT = 207 * K + 512
    with contextlib.ExitStack() as st:
        ar = st.enter_context(nc.sbuf_tensor("arena", [128, LIMIT], U8))
        banks = [st.enter_context(nc.psum_tensor("bank%d" % i, [128, 512], F32)) for i in range(8)]
        A = Arena(ar, LIMIT)
        o = 0
        IDF = A.f32("IDF", o, 128, 0, 9); o += 512
        ONES = A.f32("ONES", o, 128, 0, 9); o += 512
        IDB = A.bf("IDB", o, 128, 0, 9); o += 256
        VEC = A.f32("VEC", o, 64, 0, 9); o += 256
        DWWT = A.f32("DWWT", o, 128, 0, 9); o += 512
        CACT = A.bf("CACT", o, 16, 0, 9); o += 32
        CACTF = A.f32("CACTF", o, 8, 0, 9); o += 32
        EPSC = A.f32("EPSC", o, 8, 0, 9); o += 32
        FJ = A.f32("FJ", o, 32, 0, 9); o += 128
        AB = A.f32("AB", o, 32, 0, 9); o += 128
        SS = A.f32("SS", o, 48, 0, 9); o += 192
        RS = A.f32("RS", o, 48, 0, 9); o += 192
        NEGH = A.f32("NEGH", o, 512, 0, 9); o += 2048
        RCNT = A.f32("RCNT", o, 64, 0, 9); o += 256
        JUNK = A.bf("JUNK", o, 1024, 0, 9); o += 2048
        GT1B_OFF = o
        GT1B = A.f32("GT1B", o, 1024, 0, 2); o += 4096
        GT2B = A.f32("GT2B", o, 1024, 0, 9); o += 4096
        GFB = A.f32("GFB", o, 1024, 0, 9); o += 4096
        assert o <= 20 * K, o
        RX = 20 * K
        RH = RX + 64 * K
        RY = RH + 33280
        RW = RY + 33 * K + 512
        SS3 = SS.rearrange("p (a b) -> p a b", a=3)
        RS3 = RS.rearrange("p (a b) -> p a b", a=3)
        RCNT3 = RCNT.rearrange("p (a b) -> p a b", a=4)
        o = RX
        WIN = A.bf("WIN", o, 8 * 1536, 0, 1).rearrange("p (k n) -> p k n", k=8); o += 24 * K
        HT = [A.bf("HT%d" % i, o + i * 8 * K, 8 * 512, 1, 1).rearrange("p (k n) -> p k n", k=8) for i in range(2)]; o += 16 * K
        NXIN = 3
        XIN = [A.f32("XIN%d" % i, o + i * 4 * K, 1024, 0, 1) for i in range(NXIN)]; o += NXIN * 4 * K
        HB = [A.bf("H%d" % i, o + i * 2 * K, 1024, 1, 1) for i in range(2)]; o += 4 * K
        SIG = [A.f32("SIG%d" % i, o + i * 2 * K, 512, 1, 1) for i in range(1)]; o += 2 * K
        T12 = [A.f32("T%d" % i, o + i * 2112, 528, 1, 1) for i in range(2)]; o += 4224 + 32 * 0
        o = (o + 31) // 32 * 32
        assert o <= RH, (o, RH)
        GLW = 30 + SEQ + 2
        GLU = A.bf("GLU", RH, 4 * GLW, 0, 2).rearrange("p (c n) -> p c n", c=4)
        PP = A.bf("PP", RH + 4 * GLW * 2, 4 * SEQ, 1, 2).rearrange("p (c n) -> p c n", c=4)
        assert 4 * GLW * 2 + 4 * SEQ * 2 <= 33280
        o = RY
        VV = [A.f32("V%d" % i, o + i * 8448, 4 * 528, 1, 1).rearrange("p (c n) -> p c n", c=4) for i in range(2)]; o += 2 * 8448
        WA = [A.bf("WA%d" % i, o + i * 4 * K, 8 * 256, 0, 1).rearrange("p (k n) -> p k n", k=8) for i in range(4)]; o += 16 * K
        assert o <= RW, (o, RW)
        o = RW
        DIAG = A.bf("DIAG", o, 124 * 128, 0, 2).rearrange("p (j n) -> p j n", j=124); o += 31 * K
        WPW = A.bf("WPW", o, 4 * 512, 0, 2).rearrange("p (k n) -> p k n", k=4); o += 4 * K
        WPG = A.bf("WPG", o, 4 * 128, 0, 2).rearrange("p (k n) -> p k n", k=4); o += 1 * K
        RWT = o
        MODROW = A.f32("MODROW", o, 1024, 0, 1); o += 4 * K
        BADA = A.f32("BADA", o, 1024, 0, 1); o += 4 * K
        GFROW = A.f32("GFROW", o, 1024, 0, 1); o += 4 * K
        BADA1 = GFROW
        RBUF = [A.f32("RBUF%d" % i, o + i * 2112, 528, 1, 1) for i in range(4)]; o += 8448
        assert o <= LIMIT
        o = RX
        CONV = [A.f32("CONV%d" % i, o + i * 8 * K, 4 * 512, 2, 2).rearrange("p (c n) -> p c n", c=4) for i in range(2)]; o += 16 * K
        SQ = A.f32("SQ", o, 4 * 512, 2, 2).rearrange("p (c n) -> p c n", c=4); o += 8 * K
        MUB = A.f32("MUB", o, 512, 2, 2); o += 2 * K
        RSB = A.f32("RSB", o, 512, 2, 2); o += 2 * K
        TMB = A.f32("TMB", o, 512, 2, 2); o += 2 * K
        LT = [A.f32("LT%d" % i, o + i * 2 * K, 512, 2, 2) for i in range(2)]; o += 4 * K
        ZT = [A.bf("ZT%d" % i, o + i * 4 * K, 4 * 512, 2, 2).rearrange("p (c n) -> p c n", c=4) for i in range(2)]; o += 8 * K
        WA2 = [A.bf("WA2%d" % i, o + i * 4 * K, 8 * 256, 2, 2).rearrange("p (k n) -> p k n", k=8) for i in range(3)]; o += 12 * K
        MODROW2 = A.f32("MODROW2", o, 1024, 2, 2); o += 4 * K
        assert o <= RX + 60 * K, o
        YT = A.bf("YT", RY, 8 * SEQ, 2, 3).rearrange("p (k n) -> p k n", k=8)
        X = A.f32("X", RX, NT * D, 3, 4).rearrange("p (t d) -> p t d", t=NT)
        H2T = A.bf("H2T", RH, 8 * SEQ, 3, 4).rearrange("p (k n) -> p k n", k=8)
        o = RWT
        WOUT = A.bf("WOUT", o, 8 * D, 2, 3).rearrange("p (k n) -> p k n", k=8); o += 16 * K
        WOST = [A.f32("WOST%d" % i, o + i * 2 * K, 512, 2, 3) for i in range(2)]; o += 4 * K
        H2B = [A.bf("H2B%d" % i, GT1B_OFF + i * 2 * K, 1024, 3, 3) for i in range(2)]
        assert o <= LIMIT, (o, LIMIT)
        ACT_ = A.bf("A", RY, GMAX * SEQ, 4, 4).rearrange("p (j n) -> p j n", j=GMAX)
        o = RW
        WG = [A.bf("WG%d" % i, o + i * 2 * K, 8 * 128, 3, 4).rearrange("p (k n) -> p k n", k=8) for i in range(3)]; o += 6 * K
        WU = [A.bf("WU%d" % i, o + i * 2 * K, 8 * 128, 3, 4).rearrange("p (k n) -> p k n", k=8) for i in range(3)]; o += 6 * K
        WDST = [A.f32("WDST%d" % i, o + i * 4 * K, 1024, 3, 4) for i in range(2)]; o += 8 * K
        WD0 = A.bf("WD0", o, GMAX * D, 3, 4).rearrange("p (j n) -> p j n", j=GMAX); o += 12 * K
        SG = [A.f32("SG%d" % i, o + i * 2 * K, 512, 3, 4) for i in range(2)]; o += 4 * K
        assert o <= RWT, (o, RWT)
        WD1 = A.bf("WD1", RWT, GMAX * D, 4, 4).rearrange("p (j n) -> p j n", j=GMAX)
        WD = [WD0, WD1]

        S = Sched(nc)
        pbank = [0]

        busy_banks = set()

        def nb():
            while True:
                b = pbank[0]
                pbank[0] = (b + 1) % 8
                if b not in busy_banks:
                    return b

        def bankbf(b):
            return banks[b][:].bitcast(BF16).rearrange("p (k t) -> p k t", k=8)

        wa_v = w_ada.rearrange("(k p) n -> p k n", p=128)

        class AdaCtx:
            def __init__(self, was, ncols, modrow, badas, tag):
                self.was, self.ncols, self.modrow, self.badas, self.tag, self.cnt = was, ncols, modrow, badas, tag, 0
                self.bias_of = {}

        def ada_dma(ctx, s, p):
            sl = ctx.cnt % len(ctx.was)
            ctx.cnt += 1
            n = ctx.ncols
            c0 = s * 1024 + p * n
            S.add("pool", lambda e: e.dma_start(out=ctx.was[sl][:, :, :], in_=wa_v[:, :, c0:c0 + n]),
                  writes=[(ctx.tag + "WA", sl)], dma=(ctx.tag + "wa", sl))
            if ctx.tag == "2":
                ctx.bias_of[(s, p)] = None
                if p == 0:
                    S.add("sp", lambda e: e.dma_start(out=ctx.modrow[0:1, :], in_=b_ada[:, s * 1024:(s + 1) * 1024]),
                          writes=[(ctx.tag + "MODROW", q) for q in range(1024 // n)], dma=(ctx.tag + "bada", 0))
            else:
                bi = s % len(ctx.badas)
                ctx.bias_of[(s, p)] = (bi, p * n)
                if p == 0:
                    S.add("sp", lambda e: e.dma_start(out=ctx.badas[bi][0:1, :], in_=b_ada[:, s * 1024:(s + 1) * 1024]),
                          writes=[(ctx.tag + "BADA", bi)] + ([("GFROW",)] if bi == 1 else []),
                          dma=(ctx.tag + "bada", bi))
            return sl

        def m1_load(i):
            sl = i % NXIN
            S.add("sp", lambda e: e.dma_start(out=XIN[sl][:, :], in_=x_d[i * 128:(i + 1) * 128, :]),
                  writes=[("XIN", sl)], dma=("xin", sl))

        S.add("sp", lambda e: e.dma_start(out=VEC[:, 0:40], in_=vec_d[:, :]), writes=[("VEC",)], dma=("vec",))
        for i_ in range(NXIN):
            m1_load(i_)
        ctx1 = AdaCtx(WA, 256, MODROW, [BADA, BADA1], "")
        pre_sl = [ada_dma(ctx1, 0, p_) for p_ in range(4)]
        S.add("sp", lambda e: e.dma_start(out=GFROW[0:1, :], in_=gf_d[:, :]), writes=[("GFROW",)], dma=("gfrow",))
        S.add("sp", lambda e: e.dma_start(out=DWWT[:, 0:124], in_=dwwt_d[:, :]), writes=[("DWWT",)], dma=("dwwt",))
        win_v = w_in.rearrange("(k p) n -> p k n", p=128)

        def win_load(blk):
            S.add("pool", lambda e: e.dma_start(out=WIN[:, :, blk * 512:(blk + 1) * 512], in_=win_v[:, :, blk * 512:(blk + 1) * 512]),
                  writes=[("WIN", blk)], dma=("win", blk))
        S.add("pool", lambda e: e.memset(IDF, 0.0), writes=[("IDF",)])
        S.add("pool", lambda e: e.affine_select(out=IDF, in_=IDF, compare_op=ALU.not_equal, fill=1.0, base=0,
                                                 pattern=[[-1, 128]], channel_multiplier=1),
              reads=[("IDF",)], writes=[("IDF",)])
        S.add("pool", lambda e: e.tensor_copy(IDB, IDF), reads=[("IDF",)], writes=[("IDB",)])
        S.add("pool", lambda e: e.memset(ONES, 1.0), writes=[("ONES",)])
        S.add("pool", lambda e: e.memset(NEGH, -0.5), writes=[("NEGH",)])
        S.add("pool", lambda e: e.memset(EPSC, EPS), writes=[("EPSC",)])
        S.add("pool", lambda e: e.memset(SS, 0.0), writes=[("SS", w_, i_) for w_ in range(3) for i_ in range(NT)])
        S.add("pool", lambda e: e.memset(GLU[:, :, 0:30], 0.0), writes=[("GLUPAD",)])
        S.add("pool", lambda e: e.memset(VV[1][:, :, 512:528], 0.0), writes=[("V", 1, "tail")])

        def scratch_init(e):
            ins = None
            for t_ in T12 + RBUF:
                ins = e.memset(t_[:, 0:16], 0.0)
            return ins
        S.add("pool", scratch_init, writes=[("T", 0), ("T", 1)] + [("R", q) for q in range(4)])

        def cnt_fill(e):
            ins = None
            for g in range(4):
                w = 2 << g
                for t in range(16):
                    ins = e.memset(RCNT3[:, g, t:t + 1], 1.0 / min(t + 1, w))
            return ins
        S.add("pool", cnt_fill, writes=[("RCNT",)])

        S.add("act", lambda e: e.activation(CACTF[:, 0:8], VEC[:, 0:8], AF.Silu), reads=[("VEC",)], writes=[("CACTF",)])
        S.add("dve", lambda e: e.tensor_copy(CACT[:, 0:8], CACTF[:, 0:8]), reads=[("CACTF",)], writes=[("CACT",)])
        S.add("act", lambda e: e.activation(FJ[:, 24:25], EPSC[:, 0:1], AF.Sigmoid), reads=[("EPSC",)], writes=[("FJ", 24)])

        def ada_mm(ctx, s, p, sl):
            n = ctx.ncols
            b = nb()

            def mm(e):
                ins = None
                for k in range(8):
                    ins = e.matmul(banks[b][0:1, 0:n], CACT[:, k:k + 1], ctx.was[sl][:, k, :], start=(k == 0), stop=(k == 7))
                return ins
            S.add("pe", mm, reads=[("CACT",), (ctx.tag + "WA", sl)], writes=[("ps", b)])
            if ctx.bias_of[(s, p)] is None:
                S.add("dve", lambda e: e.tensor_tensor(ctx.modrow[0:1, p * n:(p + 1) * n], banks[b][0:1, 0:n],
                                                         ctx.modrow[0:1, p * n:(p + 1) * n], ALU.add),
                      reads=[("ps", b), (ctx.tag + "MODROW", p)], writes=[(ctx.tag + "MODROW", p)])
                return
            bi, boff = ctx.bias_of[(s, p)]
            S.add("dve", lambda e: e.tensor_tensor(ctx.modrow[0:1, p * n:(p + 1) * n], banks[b][0:1, 0:n],
                                                     ctx.badas[bi][0:1, boff:boff + n], ALU.add),
                  reads=[("ps", b), (ctx.tag + "BADA", bi)], writes=[(ctx.tag + "MODROW", p)])

        def row_to_cols(ctx):
            b = nb()
            MODROW = ctx.modrow

            def mm(e):
                ins = None
                for c in range(8):
                    ins = e.matmul(banks[b][:, c:c + 1], MODROW[0:1, c * 128:(c + 1) * 128], ONES[0:1, 0:1],
                                   start=True, stop=True)
                return ins
            S.add("pe", mm, reads=[(ctx.tag + "MODROW", p) for p in range(1024 // ctx.ncols)] + [("ONES",)], writes=[("ps", b)])
            return b

        def row_bcast(row, rkeys, dst, dkey):
            for hf in range(2):
                b = nb()
                S.add("pe", (lambda b, hf: lambda e: e.matmul(banks[b][:, :], ONES[0:1, :], row[0:1, hf * 512:(hf + 1) * 512],
                                                               start=True, stop=True))(b, hf),
                      reads=list(rkeys) + [("ONES",)], writes=[("ps", b)])
                S.add("dve", (lambda b, hf: lambda e: e.tensor_copy(dst[:, hf * 512:(hf + 1) * 512], banks[b][:, :]))(b, hf),
                      reads=[("ps", b)], writes=[(dkey, hf)])

        def ada_finish(ctx, s):
            MK = [(ctx.tag + "MODROW", p) for p in range(1024 // ctx.ncols)]
            if s == 0:
                b = row_to_cols(ctx)
                S.add("dve", lambda e: e.tensor_copy(AB[:, 8:16], banks[b][:, 0:8]), reads=[("ps", b)], writes=[("AB", 1)])
            elif s == 1:
                b = row_to_cols(ctx)
                S.add("dve", lambda e: e.scalar_tensor_tensor(AB[:, 0:8], banks[b][:, 0:8], 1.0, VEC[:, 8:16], ALU.add, ALU.mult),
                      reads=[("ps", b), ("VEC",)], writes=[("AB", 0)])
            elif s == 2:
                row_bcast(ctx.modrow, MK, GT1B, "GT1B")
            elif s == 3:
                b = row_to_cols(ctx)
                S.add("dve", lambda e: e.tensor_copy(AB[:, 24:32], banks[b][:, 0:8]), reads=[("ps", b)], writes=[("AB", 3)])
            elif s == 4:
                b = row_to_cols(ctx)
                S.add("dve", lambda e: e.scalar_tensor_tensor(AB[:, 16:24], banks[b][:, 0:8], 1.0, VEC[:, 16:24], ALU.add, ALU.mult),
                      reads=[("ps", b), ("VEC",)], writes=[("AB", 2)])
            else:
                row_bcast(ctx.modrow, MK, GT2B, "GT2B")

        row_bcast(GFROW, [("GFROW",)], GFB, "GFB")
        for p_ in range(4):
            ada_mm(ctx1, 0, p_, pre_sl[p_])
        ada_finish(ctx1, 0)
        sl1 = [ada_dma(ctx1, 1, p_) for p_ in range(4)]
        for blk in (1, 0, 2):
            win_load(blk)
        for p_ in range(4):
            ada_mm(ctx1, 1, p_, sl1[p_])
        ada_finish(ctx1, 1)

        def diag_build(i0, i1):
            def f(e):
                ins = None
                for idx in range(i0, i1):
                    if idx % 31 < 6:
                        continue
                    ins = e.tensor_tensor(DIAG[:, idx, :], IDF[:, :], DWWT[:, idx:idx + 1].to_broadcast([128, 128]), ALU.mult)
                return ins
            S.add("pool", f, reads=[("IDF",), ("DWWT",)], writes=[("DIAGP", i0 // 8)])
        DIAGK = [("DIAGP", q) for q in range(16)]

        def normA(i, which, src_ap, src_keys, hdst, hkey, defer_b=False):
            S.add("act", lambda e: e.activation(JUNK[:, :], src_ap, AF.Square, accum_out=SS3[:, which, i:i + 1]),
                  reads=src_keys, writes=[("JUNK",), ("SS", which, i)])
            S.add("dve", lambda e: e.tensor_scalar(RS3[:, which, i:i + 1], SS3[:, which, i:i + 1], 1.0 / D, EPS, ALU.mult, ALU.add),
                  reads=[("SS", which, i)], writes=[("RS", which, i)])
            S.add("pool", lambda e: e.tensor_tensor(RS3[:, which, i:i + 1], RS3[:, which, i:i + 1], NEGH[:, 0:1], ALU.pow),
                  reads=[("RS", which, i), ("NEGH",)], writes=[("RS", which, i)])
            def part_b():
                S.add("dve", lambda e: e.tensor_scalar(hdst, src_ap, RS3[:, which, i:i + 1], None, ALU.mult),
                      reads=src_keys + [("RS", which, i)], writes=[hkey])
            if hdst is None:
                return None
            if defer_b:
                return part_b
            part_b()
            return None

        def transposeT(hsrc, hkey, abcol, dst_fn, dkey, act_only=False):
            if act_only:
                b0_ = nb()
                bv0 = bankbf(b0_)

                def tr0(e):
                    ins = None
                    for k in range(8):
                        ins = e.transpose(bv0[:, k, :], hsrc[:, k * 128:(k + 1) * 128], IDB[:, :])
                    return ins
                S.add("pe", tr0, reads=[hkey, ("IDB",)], writes=[("ps", b0_)])

                def ev0(e):
                    ins = None
                    for k in range(8):
                        ins = e.activation(dst_fn(k), bv0[:, k, :], AF.Identity, bias=AB[:, abcol + 8 + k:abcol + 9 + k],
                                           scale=AB[:, abcol + k:abcol + k + 1])
                    return ins
                S.add("act", ev0, reads=[("ps", b0_), ("AB", abcol // 8), ("AB", abcol // 8 + 1)], writes=[dkey + ("a",), dkey + ("b",)])
                return
            b1_ = nb()
            b2_ = nb()
            bv1 = bankbf(b1_)
            bv2 = bankbf(b2_)

            def tr1(e):
                ins = None
                for k in range(0, 5):
                    ins = e.transpose(bv1[:, k, :], hsrc[:, k * 128:(k + 1) * 128], IDB[:, :])
                return ins

            def tr2(e):
                ins = None
                for k in range(5, 8):
                    ins = e.transpose(bv2[:, k, :], hsrc[:, k * 128:(k + 1) * 128], IDB[:, :])
                return ins
            S.add("pe", tr1, reads=[hkey, ("IDB",)], writes=[("ps", b1_)])
            S.add("pe", tr2, reads=[hkey, ("IDB",)], writes=[("ps", b2_)])

            def ev(e):
                ins = None
                for k in range(0, 5):
                    ins = e.activation(dst_fn(k), bv1[:, k, :], AF.Identity, bias=AB[:, abcol + 8 + k:abcol + 9 + k],
                                       scale=AB[:, abcol + k:abcol + k + 1])
                return ins

            def ev2(e):
                ins = None
                for k in range(5, 8):
                    ins = e.tensor_scalar(dst_fn(k), bv2[:, k, :], AB[:, abcol + k:abcol + k + 1], AB[:, abcol + 8 + k:abcol + 9 + k],
                                          ALU.mult, ALU.add)
                return ins
            S.add("act", ev, reads=[("ps", b1_), ("AB", abcol // 8), ("AB", abcol // 8 + 1)], writes=[dkey + ("a",)])
            S.add("dve", ev2, reads=[("ps", b2_), ("AB", abcol // 8), ("AB", abcol // 8 + 1)], writes=[dkey + ("b",)])

        def m1_normA(i):
            sl = i % 2
            xs = i % NXIN
            return normA(i, 0, XIN[xs][:, :], [("XIN", xs)], HB[sl][:, :], ("H", sl), defer_b=True)

        def m1_T(i):
            sl = i % 2
            st_, q = divmod(i, 4)
            hs = st_ % 2
            transposeT(HB[sl], ("H", sl), 0, lambda k: HT[hs][:, k, q * 128:(q + 1) * 128], ("HT", hs, q), act_only=(i >= 4))

        def p2_chunk(st_, kind, cc):
            hs = st_ % 2
            m = {"g": 4 + cc, "a": cc, "p": 8 + cc}[kind]
            b = nb()

            def mm(e):
                ins = None
                for k in range(8):
                    ins = e.matmul(banks[b][:, :], WIN[:, k, m * 128:(m + 1) * 128], HT[hs][:, k, :],
                                   start=(k == 0), stop=(k == 7))
                return ins
            S.add("pe", mm, reads=[("WIN", m // 4)] + [("HT", hs, q, h_) for q in range(4) for h_ in "ab"], writes=[("ps", b)])
            if kind == "g":
                sl = 0
                S.add("act", lambda e: e.activation(SIG[sl][:, :], banks[b][:, :], AF.Sigmoid),
                      reads=[("ps", b)], writes=[("SIG", sl)])
            elif kind == "a":
                sl = 0
                S.add("dve", lambda e: e.tensor_tensor(GLU[:, cc, 30 + st_ * 512:30 + (st_ + 1) * 512], banks[b][:, :], SIG[sl][:, :], ALU.mult),
                      reads=[("ps", b), ("SIG", sl)], writes=[("GLU", cc, st_)])
            else:
                vs = st_ % 2
                S.add("act", lambda e: e.copy(VV[vs][:, cc, 16:528], banks[b][:, :]),
                      reads=[("ps", b)], writes=[("V", vs, cc)])

        def pool_branch(st_, g):
            vs = st_ % 2
            V = VV[vs]
            Vp = VV[1 - vs]
            S.add("dve", lambda e: e.tensor_copy(V[:, g, 0:16], Vp[:, g, 512:528]),
                  reads=[("V", 1 - vs, g), ("V", 1 - vs, "tail")], writes=[("V", vs, g, "halo")])
            rk = [("V", vs, g), ("V", vs, g, "halo")]
            rb = g
            chain = [(V[:, g, :], None)]
            for step in range(g):
                chain.append((T12[step % 2][:, :], ("T", step % 2)))
            chain.append((RBUF[rb][:, :], ("R", rb)))
            sh = 1
            for step in range(g + 1):
                a_src, ak = chain[step]
                d, dk = chain[step + 1]
                S.add("dve", (lambda d, a_src, sh: lambda e: e.tensor_tensor(d[:, sh:528], a_src[:, sh:528], a_src[:, 0:528 - sh], ALU.add))(d, a_src, sh),
                      reads=rk + ([ak] if ak else []), writes=[dk])
                sh *= 2
            w = 2 << g
            sw = RBUF[rb]
            tk = ("R", rb)

            def fin():
                S.add("dve", lambda e: e.scalar_tensor_tensor(PP[:, g, st_ * 512:(st_ + 1) * 512], sw[:, 16:528], 1.0 / w, V[:, g, 16:528],
                                                              ALU.mult, ALU.subtract),
                      reads=[tk] + rk, writes=[("PP", g, st_)])
                if st_ == 0:
                    S.add("dve", lambda e: e.tensor_tensor(sw[:, 16:32], sw[:, 16:32], RCNT3[:, g, :], ALU.mult),
                          reads=[tk, ("RCNT",)], writes=[tk])
                    S.add("dve", lambda e: e.tensor_tensor(PP[:, g, 0:16], sw[:, 16:32], V[:, g, 16:32], ALU.subtract),
                          reads=[tk] + rk, writes=[("PP", g, st_)])
            return fin

        m1_normA(0)()
        pend = []
        late = []
        cur_i = [0]
        for i in range(NT):
            cur_i[0] = i
            nb_ = m1_normA(i + 1) if i + 1 < NT else None
            if i == 0 and nb_:
                nb_()
                nb_ = None
            m1_T(i)
            if nb_:
                nb_()
            if i + NXIN < NT:
                m1_load(i + NXIN)
            for _ in range(3):
                if pend:
                    pend.pop(0)()
            while late and late[0][0] <= i:
                late.pop(0)[1]()
            if i % 4 == 3:
                st_ = i // 4
                for cc in range(4):
                    pend.append((lambda st_, cc: lambda: p2_chunk(st_, "g", cc))(st_, cc))
                    pend.append((lambda st_, cc: lambda: p2_chunk(st_, "a", cc))(st_, cc))
                for g in range(4):
                    pend.append((lambda st_, g: lambda: (p2_chunk(st_, "p", g), late.append((cur_i[0], pool_branch(st_, g)))))(st_, g))
                if st_ == 0:
                    bw = nb()

                    def warm(e):
                        ins = None
                        for q in range(28):
                            ins = e.matmul(banks[bw][:, 0:128], IDB[:, :], IDB[:, :], start=True, stop=True)
                        return ins
                    S.add("pe", warm, reads=[("IDB",)], writes=[("ps", bw)])
                for _ in range(2):
                    pend.pop(0)()
            if i * 8 < 124:
                diag_build(i * 8, min(124, i * 8 + 8))
            if i == 8:
                wpw_v = w_pw.rearrange("(k p) n -> p k n", p=128)
                S.add("pool", lambda e: e.dma_start(out=WPW[:, :, :], in_=wpw_v[:, :, :]), writes=[("WPW",)], dma=("wpw",))
                wpg_v = w_pg.rearrange("g c d -> c g d")
                S.add("pool", lambda e: e.dma_start(out=WPG[:, :, :], in_=wpg_v[:, :, :]), writes=[("WPG",)], dma=("wpg",))
        i = NT
        while pend or late:
            cur_i[0] = i
            for _ in range(3):
                if pend:
                    pend.pop(0)()
            while late and (late[0][0] <= i or not pend):
                late.pop(0)[1]()
            i += 1

        S.fence("pool", lambda e: e.memset(FJ[:, 0:2], 0.0),
                ["WIN", "HT", "XIN", "H", "SIG", "T"],
                ["CONV", "SQ", "MUB", "RSB", "TMB", "LT", "ZT", "2WA", "2MODROW", "2BADA", "XPRE"])
        S.fence("pool", lambda e: e.memset(FJ[:, 2:4], 0.0), ["V", "WA"], ["YT"])
        M4ORD = list(range(NT))
        S.add("sp", lambda e: e.dma_start(out=X[:, NT - 1, :], in_=x_d[(NT - 1) * 128:NT * 128, :]),
              writes=[("X", NT - 1, 0), ("X", NT - 1, 1), ("XPRE",)], dma=("xld", 0))
        S.fence("pool", lambda e: e.memset(FJ[:, 4:6], 0.0), ["MODROW", "BADA", "GFROW", "R"], ["WOUT", "WOST"])

        def wout_prep(k, hf):
            sl = (k * 2 + hf) % 2
            S.add("sp", lambda e: e.dma_start(out=WOST[sl][:, :], in_=w_out[k * 128:(k + 1) * 128, hf * 512:(hf + 1) * 512]),
                  writes=[("WOST", sl)], dma=("wost", sl))
            S.add("pool", lambda e: e.tensor_tensor(WOUT[:, k, hf * 512:(hf + 1) * 512], WOST[sl][:, :], GT1B[:, hf * 512:(hf + 1) * 512], ALU.mult),
                  reads=[("WOST", sl), ("GT1B", hf)], writes=[("WOUT", k, hf)])

        wprep = [(k, hf) for k in range(8) for hf in range(2)]

        NDT = 6

        def m3_taps(st_):
            cs = st_ % 2
            for j in range(NDT):
                for cc in range(4):
                    gk = [("GLUPAD",), ("GLU", cc, st_)] + ([("GLU", cc, st_ - 1)] if st_ > 0 else [])
                    if j == 0:
                        S.add("dve", (lambda cc: lambda e: e.tensor_scalar(
                            CONV[cs][:, cc, :], GLU[:, cc, st_ * 512:st_ * 512 + 512], DWWT[:, cc * 31:cc * 31 + 1],
                            VEC[:, 24 + cc:25 + cc], ALU.mult, ALU.add))(cc),
                            reads=gk + [("DWWT",), ("VEC",)], writes=[("CONV", cs, cc)])
                    else:
                        S.add("dve", (lambda cc, j: lambda e: e.scalar_tensor_tensor(
                            CONV[cs][:, cc, :], GLU[:, cc, st_ * 512 + j:st_ * 512 + j + 512],
                            DWWT[:, cc * 31 + j:cc * 31 + j + 1], CONV[cs][:, cc, :], ALU.mult, ALU.add))(cc, j),
                            reads=gk + [("DWWT",), ("CONV", cs, cc)], writes=[("CONV", cs, cc)])

        def m3_conv(st_, cc):
            cs = st_ % 2
            b = nb()
            gk = [("GLUPAD",), ("GLU", cc, st_)] + ([("GLU", cc, st_ - 1)] if st_ > 0 else [])

            def mm(e):
                ins = None
                for j in range(NDT, KW):
                    ins = e.matmul(banks[b][:, :], DIAG[:, cc * 31 + j, :], GLU[:, cc, st_ * 512 + j:st_ * 512 + j + 512],
                                   start=(j == NDT), stop=(j == KW - 1))
                return ins
            S.add("pe", mm, reads=DIAGK + gk, writes=[("ps", b)])
            S.add("dve", lambda e: e.tensor_tensor(CONV[cs][:, cc, :], CONV[cs][:, cc, :], banks[b][:, :], ALU.add),
                  reads=[("ps", b), ("CONV", cs, cc)], writes=[("CONV", cs, cc)])
            S.add("act", lambda e: e.activation(SQ[:, cc, :], CONV[cs][:, cc, :], AF.Square),
                  reads=[("CONV", cs, cc)], writes=[("SQ", cc)])

        def m3_ln(st_):
            cs = st_ % 2
            zs = st_ % 2
            b1_ = nb()
            b2_ = nb()

            def mm1(e):
                ins = None
                for cc in range(4):
                    ins = e.matmul(banks[b1_][:, :], ONES[:, :], CONV[cs][:, cc, :], start=(cc == 0), stop=(cc == 3))
                return ins

            def mm2(e):
                ins = None
                for cc in range(4):
                    ins = e.matmul(banks[b2_][:, :], ONES[:, :], SQ[:, cc, :], start=(cc == 0), stop=(cc == 3))
                return ins
            S.add("pe", mm1, reads=[("CONV", cs, cc) for cc in range(4)] + [("ONES",)], writes=[("ps", b1_)])
            S.add("pe", mm2, reads=[("SQ", cc) for cc in range(4)] + [("ONES",)], writes=[("ps", b2_)])
            S.add("dve", lambda e: e.tensor_scalar(MUB[:, :], banks[b1_][:, :], 1.0 / CW, None, ALU.mult),
                  reads=[("ps", b1_)], writes=[("MUB",)])
            S.add("dve", lambda e: e.tensor_tensor(TMB[:, :], MUB[:, :], MUB[:, :], ALU.mult), reads=[("MUB",)], writes=[("TMB",)])
            S.add("dve", lambda e: e.scalar_tensor_tensor(RSB[:, :], banks[b2_][:, :], 1.0 / CW, TMB[:, :], ALU.mult, ALU.subtract),
                  reads=[("ps", b2_), ("TMB",)], writes=[("RSB",)])
            S.add("dve", lambda e: e.tensor_scalar(RSB[:, :], RSB[:, :], 0.0, None, ALU.max), reads=[("RSB",)], writes=[("RSB",)])
            S.add("act", lambda e: e.activation(TMB[:, :], RSB[:, :], AF.Ln, bias=EPSC[:, 0:1]),
                  reads=[("RSB",), ("EPSC",)], writes=[("TMB",)])
            S.add("act", lambda e: e.activation(RSB[:, :], TMB[:, :], AF.Exp, scale=-0.5),
                  reads=[("TMB",)], writes=[("RSB",)])
            for cc in range(4):
                ls = cc % 2
                S.add("dve", (lambda cc, ls: lambda e: e.tensor_tensor(LT[ls][:, :], CONV[cs][:, cc, :], MUB[:, :], ALU.subtract))(cc, ls),
                      reads=[("CONV", cs, cc), ("MUB",)], writes=[("LT", ls)])
                S.add("dve", (lambda cc, ls: lambda e: e.tensor_tensor(LT[ls][:, :], LT[ls][:, :], RSB[:, :], ALU.mult))(cc, ls),
                      reads=[("LT", ls), ("RSB",)], writes=[("LT", ls)])
                S.add("act", (lambda cc, ls: lambda e: e.activation(ZT[zs][:, cc, :], LT[ls][:, :], AF.Silu,
                                                                     bias=VEC[:, 32 + cc:33 + cc], scale=VEC[:, 28 + cc:29 + cc]))(cc, ls),
                      reads=[("LT", ls), ("VEC",)], writes=[("ZT", zs, cc)])

        def m3_pw(st_, m):
            zs = st_ % 2
            b = nb()

            def mm(e):
                ins = None
                for cc in range(4):
                    ins = e.matmul(banks[b][:, :], WPW[:, cc, m * 128:(m + 1) * 128], ZT[zs][:, cc, :], start=(cc == 0), stop=(cc == 3))
                return ins
            S.add("pe", mm, reads=[("WPW",)] + [("ZT", zs, cc) for cc in range(4)], writes=[("ps", b)])
            S.add("act", lambda e: e.copy(YT[:, m, st_ * 512:(st_ + 1) * 512], banks[b][:, :]),
                  reads=[("ps", b)], writes=[("YT", m, st_)])

        def m3_pg(st_, g):
            b = nb()
            S.add("pe", lambda e: e.matmul(banks[b][:, :], WPG[:, g, :], PP[:, g, st_ * 512:(st_ + 1) * 512], start=True, stop=True),
                  reads=[("WPG",), ("PP", g, st_)], writes=[("ps", b)])
            S.add("act", lambda e: e.activation(YT[:, 4 + g, st_ * 512:(st_ + 1) * 512], banks[b][:, :], AF.Identity, scale=VEC[:, 36 + g:37 + g]),
                  reads=[("ps", b), ("VEC",)], writes=[("YT", 4 + g, st_)])

        def m4_out(i, defer_adds=False):
            bs = [nb(), nb()]

            def mm(e):
                ins = None
                for k in range(8):
                    for hf in range(2):
                        ins = e.matmul(banks[bs[hf]][:, :], YT[:, k, i * 128:(i + 1) * 128], WOUT[:, k, hf * 512:(hf + 1) * 512],
                                       start=(k == 0), stop=(k == 7))
                return ins
            S.add("pe", mm, reads=[("YT", k, i // 4) for k in range(8)] + [("WOUT", k, hf) for k in range(8) for hf in range(2)],
                  writes=[("ps", bs[0]), ("ps", bs[1])])
            def adds():
                busy_banks.difference_update(bs)
                for hf in range(2):
                    S.add("dve", (lambda hf: lambda e: e.tensor_tensor(X[:, i, hf * 512:(hf + 1) * 512], X[:, i, hf * 512:(hf + 1) * 512],
                                                                       banks[bs[hf]][:, :], ALU.add))(hf),
                          reads=[("ps", bs[hf]), ("X", i, hf)], writes=[("X", i, hf)])
            if defer_adds:
                busy_banks.update(bs)
                return adds
            adds()
            return None

        ctx2 = AdaCtx(WA2, 256, MODROW2, [], "2")
        m3_taps(0)
        for cc in range(4):
            m3_conv(0, cc)
        next_sls = None
        for st_ in range(NST):
            s_ = 2 + st_
            sls = next_sls if next_sls is not None else [ada_dma(ctx2, s_, p_) for p_ in range(3)]
            next_sls = None
            if st_ + 1 < NST:
                m3_taps(st_ + 1)
            m3_ln(st_)
            if st_ + 1 < NST:
                for cc in range(4):
                    m3_conv(st_ + 1, cc)
                    ada_mm(ctx2, s_, cc, sls[cc])
                    if cc == 0:
                        sls.append(ada_dma(ctx2, s_, 3))
                ada_finish(ctx2, s_)
                next_sls = [ada_dma(ctx2, s_ + 1, p_) for p_ in range(3)]
                for g in range(4):
                    m3_pg(st_, g)
            else:
                ada_mm(ctx2, s_, 0, sls[0])
                sls.append(ada_dma(ctx2, s_, 3))
                for g in range(4):
                    m3_pg(st_, g)
                ada_mm(ctx2, s_, 1, sls[1])
                ada_mm(ctx2, s_, 2, sls[2])
                early_bank0 = pbank[0]
                early_adds = [m4_out(0, defer_adds=True), m4_out(1, defer_adds=True)]
                ada_mm(ctx2, s_, 3, sls[3])
                ada_finish(ctx2, s_)
            for m in range(4):
                m3_pw(st_, m)
            for _ in range(6):
                if wprep:
                    wout_prep(*wprep.pop(0))
        while wprep:
            wout_prep(*wprep.pop(0))

        S.fence("pool", lambda e: e.memset(FJ[:, 6:8], 0.0),
                ["CONV", "SQ", "MUB", "RSB", "TMB", "LT", "ZT", "2WA", "2MODROW", "2BADA"], ["X"])
        S.fence("pool", lambda e: e.memset(FJ[:, 8:10], 0.0), ["GLU", "GLUPAD", "PP"], ["H2T"])
        S.fence("pool", lambda e: e.memset(FJ[:, 16:18], 0.0), ["GT1B"], ["H2B"])
        S.fence("pool", lambda e: e.memset(FJ[:, 10:12], 0.0), ["DIAGP", "WPW", "WPG"], ["WG", "WU", "WDST", "WD0", "SG"])

        wg_v = w_g.rearrange("(k p) n -> p k n", p=128)
        wu_v = w_u.rearrange("(k p) n -> p k n", p=128)
        gu_count = [0]

        def gu_load(j):
            ws = gu_count[0] % 3
            gu_count[0] += 1
            S.add("pool", lambda e: e.dma_start(out=WG[ws][:, :, :], in_=wg_v[:, :, j * 128:(j + 1) * 128]),
                  writes=[("WG", ws)], dma=("wg", ws))
            S.add("pool", lambda e: e.dma_start(out=WU[ws][:, :, :], in_=wu_v[:, :, j * 128:(j + 1) * 128]),
                  writes=[("WU", ws)], dma=("wu", ws))
            return ws

        wd_count = [0]

        def wd_prep(gi, jj):
            j = GROUPS[gi][jj]
            sl = wd_count[0] % 2
            wd_count[0] += 1
            S.add("sp", lambda e: e.dma_start(out=WDST[sl][:, :], in_=w_d[j * 128:(j + 1) * 128, :]),
                  writes=[("WDST", sl)], dma=("wdst", sl))
            S.add("dve", lambda e: e.tensor_tensor(WD[gi % 2][:, jj, :], WDST[sl][:, :], GT2B[:, :], ALU.mult),
                  reads=[("WDST", sl), ("GT2B", 0), ("GT2B", 1)], writes=[("WD%d" % (gi % 2), jj)])

        def m4_load(i, n):
            S.add("sp", lambda e: e.dma_start(out=X[:, i, :], in_=x_d[i * 128:(i + 1) * 128, :]),
                  writes=[("X", i, 0), ("X", i, 1)], dma=("xld", n % 6))

        def m4_norm(i, n):
            sl = n % 2
            return normA(i, 1, X[:, i, :], [("X", i, 0), ("X", i, 1)], H2B[sl][:, :], ("H2B", sl), defer_b=True)

        def m4_T(i, n):
            sl = n % 2
            transposeT(H2B[sl], ("H2B", sl), 16, lambda k: H2T[:, k, i * 128:(i + 1) * 128], ("H2T", i))

        def ffn_gu(gi, jj, j, ws, st_):
            bg = nb()
            bu = nb()

            def mmg(e):
                ins = None
                for k in range(8):
                    ins = e.matmul(banks[bg][:, :], WG[ws][:, k, :], H2T[:, k, st_ * 512:(st_ + 1) * 512], start=(k == 0), stop=(k == 7))
                return ins

            def mmu(e):
                ins = None
                for k in range(8):
                    ins = e.matmul(banks[bu][:, :], WU[ws][:, k, :], H2T[:, k, st_ * 512:(st_ + 1) * 512], start=(k == 0), stop=(k == 7))
                return ins
            hk = [("H2T", st_ * 4 + q, h_) for q in range(4) for h_ in "ab"]
            S.add("pe", mmg, reads=[("WG", ws)] + hk, writes=[("ps", bg)])
            S.add("pe", mmu, reads=[("WU", ws)] + hk, writes=[("ps", bu)])
            sl = st_ % 2
            S.add("act", lambda e: e.activation(SG[sl][:, :], banks[bg][:, :], AF.Silu), reads=[("ps", bg)], writes=[("SG", sl)])
            S.add("dve", lambda e: e.tensor_tensor(ACT_[:, jj, st_ * 512:(st_ + 1) * 512], banks[bu][:, :], SG[sl][:, :], ALU.mult),
                  reads=[("ps", bu), ("SG", sl)], writes=[("A", jj, st_)])

        done_gu = set()
        for n in range(0, 7):
            m4_load(M4ORD[n], n)
        for a_ in early_adds:
            a_()
        pbank[0] = (early_bank0 + 4) % 8
        pre = [(lambda jj: lambda: wd_prep(0, jj))(jj) for jj in range(len(GROUPS[0]))]
        gu_slots = {}
        m4_norm(M4ORD[0], 0)()
        for n in range(NT):
            if n + 2 < NT:
                m4_out(M4ORD[n + 2])
            nb_ = m4_norm(M4ORD[n + 1], n + 1) if n + 1 < NT else None
            if n == NT - 2:
                S.fence("pool", lambda e: e.memset(FJ[:, 12:14], 0.0), ["YT"], ["A"])
                for st_ in range(2):
                    ffn_gu(0, 0, GROUPS[0][0], gu_slots[GROUPS[0][0]], st_)
                    done_gu.add((0, 0, st_))
            if n == NT - 1:
                ffn_gu(0, 0, GROUPS[0][0], gu_slots[GROUPS[0][0]], 2)
                done_gu.add((0, 0, 2))
            m4_T(M4ORD[n], n)
            if nb_:
                nb_()
            if n + 7 < NT - 1:
                m4_load(M4ORD[n + 7], n + 7)
            if n >= 8 and pre:
                pre.pop(0)()
            if n == 10:
                gu_slots[GROUPS[0][0]] = gu_load(GROUPS[0][0])
            if n == 12:
                gu_slots[GROUPS[0][1]] = gu_load(GROUPS[0][1])
        while pre:
            pre.pop(0)()

        S.fence("pool", lambda e: e.memset(FJ[:, 14:16], 0.0), ["WOUT", "WOST"], ["WD1"])
        H2K = [("H2T", i) for i in range(NT)]
        all_j = [j for g in GROUPS for j in g]

        def ffn_down(gi, i):
            nj = len(GROUPS[gi])
            bs = [nb(), nb()]

            def mm(e):
                ins = None
                for jj in range(nj):
                    for hf in range(2):
                        ins = e.matmul(banks[bs[hf]][:, :], ACT_[:, jj, i * 128:(i + 1) * 128], WD[gi % 2][:, jj, hf * 512:(hf + 1) * 512],
                                       start=(jj == 0), stop=(jj == nj - 1))
                return ins
            S.add("pe", mm, reads=[("A", jj, i // 4) for jj in range(nj)] + [("WD%d" % (gi % 2), jj) for jj in range(nj)],
                  writes=[("ps", bs[0]), ("ps", bs[1])])
            for hf in range(2):
                S.add("dve", (lambda hf: lambda e: e.tensor_tensor(X[:, i, hf * 512:(hf + 1) * 512], X[:, i, hf * 512:(hf + 1) * 512],
                                                                   banks[bs[hf]][:, :], ALU.add))(hf),
                      reads=[("ps", bs[hf]), ("X", i, hf)], writes=[("X", i, hf)])

        outs = []

        def final_a(i):
            normA(i, 2, X[:, i, :], [("X", i, 0), ("X", i, 1)], None, None)

        def final_b(i):
            S.add("dve", lambda e: e.scalar_tensor_tensor(X[:, i, :], X[:, i, :], RS3[:, 2, i:i + 1], GFB[:, :], ALU.mult, ALU.mult),
                  reads=[("X", i, 0), ("X", i, 1), ("RS", 2, i), ("GFB", 0), ("GFB", 1)], writes=[("X", i, 0), ("X", i, 1)])
            outs.append(S.add("sp", lambda e: e.dma_start(out=out_d[i * 128:(i + 1) * 128, :], in_=X[:, i, :]),
                              reads=[("X", i, 0), ("X", i, 1)], dma=("out", i % 4)))

        pos = 0
        for gi, grp in enumerate(GROUPS):
            nxt = []
            if gi + 1 < len(GROUPS):
                nxt = [(lambda gi2, jj: lambda: wd_prep(gi2, jj))(gi + 1, jj) for jj in range(len(GROUPS[gi + 1]))]
            for jj, j in enumerate(grp):
                for ahead in (pos, pos + 1, pos + 2):
                    if ahead < len(all_j) and all_j[ahead] not in gu_slots:
                        gu_slots[all_j[ahead]] = gu_load(all_j[ahead])
                ws = gu_slots[j]
                for st_ in range(NST):
                    if (gi, jj, st_) not in done_gu:
                        ffn_gu(gi, jj, j, ws, st_)
                if nxt:
                    nxt.pop(0)()
                pos += 1
            while nxt:
                nxt.pop(0)()
            last = gi == len(GROUPS) - 1
            for i in range(NT):
                ffn_down(gi, i)
                if last:
                    if i >= 1:
                        final_a(i - 1)
                    if i >= 2:
                        final_b(i - 2)
            if last:
                final_a(NT - 1)
                final_b(NT - 2)
                final_b(NT - 1)

        info = S.emit(finish_ops=outs)
    return nc, info


_CACHE = {}


def kernel(**inputs):
    f = lambda a: np.ascontiguousarray(np.asarray(a, dtype=np.float32))
    x = f(inputs["x"])
    c = f(inputs["c"])
    if "nc" not in _CACHE:
        _CACHE["nc"] = build_program()[0]
    nc = _CACHE["nc"]
    T_ = lambda a, n: f(a).reshape(n, 128).T
    dw = f(inputs["dw_w"])[0]
    dwwt = np.ascontiguousarray(dw.reshape(KW, 4, 128).transpose(2, 1, 0).reshape(128, 4 * KW))
    common = [T_(inputs["g_norm1"], 8), T_(inputs["g_norm2"], 8), T_(inputs["dw_b"], 4), T_(inputs["conv_ln_g"], 4),
              T_(inputs["conv_ln_b"], 4), T_(inputs["pool_scale"], 4)]
    shared = {
        "dwwt": dwwt,
        "w_ada": f(inputs["w_ada"])[0],
        "b_ada": f(inputs["b_ada"]).reshape(1, 6 * D),
        "w_in": f(inputs["w_in"])[0],
        "w_conv_pw": f(inputs["w_conv_pw"])[0],
        "w_pool_group": f(inputs["w_pool_group"])[0],
        "w_out": f(inputs["w_out"])[0],
        "w_ffn_gate": f(inputs["w_ffn_gate"])[0],
        "w_ffn_up": f(inputs["w_ffn_up"])[0],
        "w_ffn_down": f(inputs["w_ffn_down"])[0],
        "g_final": f(inputs["g_final"]).reshape(1, D),
    }
    in_maps = []
    for b in range(8):
        m = dict(shared)
        m["x"] = x[b]
        m["vecs"] = np.ascontiguousarray(np.concatenate([c[b].reshape(8, 128).T] + common, axis=1))
        in_maps.append(m)
    res = run_bass_kernel_spmd(nc, in_maps, core_ids=list(range(8)))
    return np.stack([np.asarray(r["out"], dtype=np.float32) for r in res.results], axis=0)
```

```python
import contextlib
import numpy as np
import concourse.bass as bass
import concourse.mybir as mybir
from concourse.bass_utils import run_bass_kernel_spmd

F32 = mybir.dt.float32
BF16 = mybir.dt.bfloat16
U8 = mybir.dt.uint8
AF = mybir.ActivationFunctionType
ALU = mybir.AluOpType
AX = mybir.AxisListType

D = 1024
SEQ = 2048
NT = 16
NST = 4
CW = 512
KW = 31
DFF = 2816
NJ = 22
EPS = 1e-6
GROUPS = [[0, 1, 2, 3, 4], [5, 6, 7, 8, 9], [10, 11, 12, 13, 14, 15], [16, 17, 18, 19, 20, 21]]
GMAX = 6
SYNC_SAME_ENGINE_WAW = True


class Op:
    __slots__ = ("eng", "fn", "is_dma", "deps", "idx", "ev", "need_ev", "semkey")


class Sched:
    ENGS = ("pe", "act", "dve", "pool", "sp")

    def __init__(self, nc):
        self.nc = nc
        self.ops = []
        self.last_w = {}
        self.readers = {}
        self.last_dma_by_key = {}
        self.buf_fence = {}

    def add(self, eng, fn, reads=(), writes=(), dma=None):
        op = Op()
        op.eng = eng
        op.fn = fn
        op.is_dma = dma is not None
        op.semkey = dma
        op.idx = len(self.ops)
        op.ev = None
        op.need_ev = False
        deps = {}

        def dep(o, raw):
            if o is not op:
                deps[o] = deps.get(o, False) or raw

        def lastw(k):
            w = self.last_w.get(k)
            if w is None:
                w = self.buf_fence.get(k[0])
            return w

        for r in reads:
            w = lastw(r)
            if w is not None:
                dep(w, True)
        for w_ in writes:
            w = lastw(w_)
            if w is not None:
                dep(w, SYNC_SAME_ENGINE_WAW)
            rd = self.readers.get(w_)
            if rd:
                for o in rd.values():
                    dep(o, SYNC_SAME_ENGINE_WAW)
        if op.is_dma:
            prev = self.last_dma_by_key.get(dma)
            if prev is not None:
                dep(prev, True)
            self.last_dma_by_key[dma] = op
        for w_ in writes:
            self.last_w[w_] = op
            self.readers[w_] = {}
        for r in reads:
            d = self.readers.setdefault(r, {})
            if op.is_dma:
                d[("dma", op.idx)] = op
            else:
                d[eng] = op
        final = []
        for o, raw in deps.items():
            if (not o.is_dma) and (not op.is_dma) and o.eng == eng:
                if eng == "pe" or not raw:
                    continue
            final.append(o)
            o.need_ev = True
        op.deps = final
        self.ops.append(op)
        return op

    def fence(self, eng, fn, old_bufs, new_bufs):
        olds = set(old_bufs)
        keys = [k for k in set(self.last_w) | set(self.readers) if k[0] in olds]
        op = self.add(eng, fn, reads=(), writes=keys)
        for b in new_bufs:
            self.buf_fence[b] = op
        return op

    def emit(self, finish_ops=()):
        nc = self.nc
        with contextlib.ExitStack() as st:
            eng_sem = {e: st.enter_context(nc.semaphore("s_" + e)) for e in self.ENGS}
            dma_keys = []
            seen = set()
            for op in self.ops:
                if op.is_dma and op.semkey not in seen:
                    seen.add(op.semkey)
                    dma_keys.append(op.semkey)
            dma_sem = {k: st.enter_context(nc.semaphore("d%d" % i)) for i, k in enumerate(dma_keys)}
            eng_cnt = {e: 0 for e in self.ENGS}
            dma_cnt = {k: 0 for k in dma_keys}
            per_eng = {e: [] for e in self.ENGS}
            for op in self.ops:
                per_eng[op.eng].append(op)
                if op.is_dma:
                    dma_cnt[op.semkey] += 16
                    op.ev = (dma_sem[op.semkey], dma_cnt[op.semkey], 16)
                elif op.need_ev:
                    eng_cnt[op.eng] += 1
                    op.ev = (eng_sem[op.eng], eng_cnt[op.eng], 1)
            fin_waits = [o.ev for o in finish_ops]
            block = st.enter_context(nc.Block())

            def run(e_handle, ename):
                known = {}
                for op in per_eng[ename]:
                    for d in op.deps:
                        sem, val, _ = d.ev
                        k = id(sem)
                        if known.get(k, 0) >= val:
                            continue
                        e_handle.wait_ge(sem, val)
                        known[k] = val
                    ins = op.fn(e_handle)
                    if op.ev is not None:
                        ins.then_inc(op.ev[0], op.ev[2])
                if ename == "sp":
                    for sem, val, _ in fin_waits:
                        if known.get(id(sem), 0) >= val:
                            continue
                        e_handle.wait_ge(sem, val)
                        known[id(sem)] = val

            @block.tensor
            def _(e):
                run(e, "pe")

            @block.scalar
            def _(e):
                run(e, "act")

            @block.vector
            def _(e):
                run(e, "dve")

            @block.gpsimd
            def _(e):
                run(e, "pool")

            @block.sync
            def _(e):
                run(e, "sp")
        return {"n_ops": len(self.ops), "eng_cnt": eng_cnt, "n_dma_sems": len(dma_keys)}


class Arena:
    def __init__(self, ar, limit):
        self.ar = ar
        self.limit = limit
        self.items = []

    def raw(self, name, off, nbytes, p0, p1):
        assert off % 32 == 0, (name, off)
        assert off + nbytes <= self.limit, (name, off, nbytes, self.limit)
        for (n2, o2, b2, q0, q1) in self.items:
            if not (p1 < q0 or q1 < p0):
                assert off + nbytes <= o2 or o2 + b2 <= off, ("overlap", name, n2)
        self.items.append((name, off, nbytes, p0, p1))
        return self.ar[:, off:off + nbytes]

    def f32(self, name, off, n, p0, p1):
        return self.raw(name, off, n * 4, p0, p1).bitcast(F32)

    def bf(self, name, off, n, p0, p1):
        return self.raw(name, off, n * 2, p0, p1).bitcast(BF16)


def build_program():
    nc = bass.Bass("TRN2", target_bir_lowering=False)
    dt_in = lambda name, shape: nc.dram_tensor(name, list(shape), F32, kind="ExternalInput").ap()
    x_d = dt_in("x", [SEQ, D])
    vec_d = dt_in("vecs", [128, 40])
    dwwt_d = dt_in("dwwt", [128, 124])
    w_ada = dt_in("w_ada", [D, 6 * D])
    b_ada = dt_in("b_ada", [1, 6 * D])
    w_in = dt_in("w_in", [D, 3 * CW])
    w_pw = dt_in("w_conv_pw", [CW, CW])
    w_pg = dt_in("w_pool_group", [4, 128, 128])
    w_out = dt_in("w_out", [D, D])
    w_g = dt_in("w_ffn_gate", [D, DFF])
    w_u = dt_in("w_ffn_up", [D, DFF])
    w_d = dt_in("w_ffn_down", [DFF, D])
    gf_d = dt_in("g_final", [1, D])
    out_d = nc.dram_tensor("out", [SEQ, D], F32, kind="ExternalOutput").ap()

    K = 1024
    LIMIT = 207 * K + 512
    with contextlib.ExitStack() as st:
        ar = st.enter_context(nc.sbuf_tensor("arena", [128, LIMIT], U8))
        banks = [st.enter_context(nc.psum_tensor("bank%d" % i, [128, 512], F32)) for i in range(8)]
        A = Arena(ar, LIMIT)
        o = 0
        IDF = A.f32("IDF", o, 128, 0, 9); o += 512
        ONES = A.f32("ONES", o, 128, 0, 9); o += 512
        IDB = A.bf("IDB", o, 128, 0, 9); o += 256
        VEC = A.f32("VEC", o, 64, 0, 9); o += 256
        DWWT = A.f32("DWWT", o, 128, 0, 9); o += 512
        CACT = A.bf("CACT", o, 16, 0, 9); o += 32
        CACTF = A.f32("CACTF", o, 8, 0, 9); o += 32
        EPSC = A.f32("EPSC", o, 8, 0, 9); o += 32
        FJ = A.f32("FJ", o, 32, 0, 9); o += 128
        AB = A.f32("AB", o, 32, 0, 9); o += 128
        SS = A.f32("SS", o, 48, 0, 9); o += 192
        RS = A.f32("RS", o, 48, 0, 9); o += 192
        NEGH = A.f32("NEGH", o, 512, 0, 9); o += 2048
        RCNT = A.f32("RCNT", o, 64, 0, 9); o += 256
        JUNK = A.bf("JUNK", o, 1024, 0, 9); o += 2048
        GT1B_OFF = o
        GT1B = A.f32("GT1B", o, 1024, 0, 2); o += 4096
        GT2B = A.f32("GT2B", o, 1024, 0, 9); o += 4096
        GFB = A.f32("GFB", o, 1024, 0, 9); o += 4096
        assert o <= 20 * K, o
        RX = 20 * K
        RH = RX + 64 * K
        RY = RH + 33280
        RW = RY + 33 * K + 512
        SS3 = SS.rearrange("p (a b) -> p a b", a=3)
        RS3 = RS.rearrange("p (a b) -> p a b", a=3)
        RCNT3 = RCNT.rearrange("p (a b) -> p a b", a=4)
        o = RX
        WIN = A.bf("WIN", o, 8 * 1536, 0, 1).rearrange("p (k n) -> p k n", k=8); o += 24 * K
        HT = [A.bf("HT%d" % i, o + i * 8 * K, 8 * 512, 1, 1).rearrange("p (k n) -> p k n", k=8) for i in range(2)]; o += 16 * K
        NXIN = 3
        XIN = [A.f32("XIN%d" % i, o + i * 4 * K, 1024, 0, 1) for i in range(NXIN)]; o += NXIN * 4 * K
        HB = [A.bf("H%d" % i, o + i * 2 * K, 1024, 1, 1) for i in range(2)]; o += 4 * K
        SIG = [A.f32("SIG%d" % i, o + i * 2 * K, 512, 1, 1) for i in range(1)]; o += 2 * K
        T12 = [A.f32("T%d" % i, o + i * 2112, 528, 1, 1) for i in range(2)]; o += 4224 + 32 * 0
        o = (o + 31) // 32 * 32
        assert o <= RH, (o, RH)
        GLW = 30 + SEQ + 2
        GLU = A.bf("GLU", RH, 4 * GLW, 0, 2).rearrange("p (c n) -> p c n", c=4)
        PP = A.bf("PP", RH + 4 * GLW * 2, 4 * SEQ, 1, 2).rearrange("p (c n) -> p c n", c=4)
        assert 4 * GLW * 2 + 4 * SEQ * 2 <= 33280
        o = RY
        VV = [A.f32("V%d" % i, o + i * 8448, 4 * 528, 1, 1).rearrange("p (c n) -> p c n", c=4) for i in range(2)]; o += 2 * 8448
        WA = [A.bf("WA%d" % i, o + i * 4 * K, 8 * 256, 0, 1).rearrange("p (k n) -> p k n", k=8) for i in range(4)]; o += 16 * K
        assert o <= RW, (o, RW)
        o = RW
        DIAG = A.bf("DIAG", o, 124 * 128, 0, 2).rearrange("p (j n) -> p j n", j=124); o += 31 * K
        WPW = A.bf("WPW", o, 4 * 512, 0, 2).rearrange("p (k n) -> p k n", k=4); o += 4 * K
        WPG = A.bf("WPG", o, 4 * 128, 0, 2).rearrange("p (k n) -> p k n", k=4); o += 1 * K
        RWT = o
        MODROW = A.f32("MODROW", o, 1024, 0, 1); o += 4 * K
        BADA = A.f32("BADA", o, 1024, 0, 1); o += 4 * K
        GFROW = A.f32("GFROW", o, 1024, 0, 1); o += 4 * K
        BADA1 = GFROW
        RBUF = [A.f32("RBUF%d" % i, o + i * 2112, 528, 1, 1) for i in range(4)]; o += 8448
        assert o <= LIMIT
        o = RX
        CONV = [A.f32("CONV%d" % i, o + i * 8 * K, 4 * 512, 2, 2).rearrange("p (c n) -> p c n", c=4) for i in range(2)]; o += 16 * K
        SQ = A.f32("SQ", o, 4 * 512, 2, 2).rearrange("p (c n) -> p c n", c=4); o += 8 * K
        MUB = A.f32("MUB", o, 512, 2, 2); o += 2 * K
        RSB = A.f32("RSB", o, 512, 2, 2); o += 2 * K
        TMB = A.f32("TMB", o, 512, 2, 2); o += 2 * K
        LT = [A.f32("LT%d" % i, o + i * 2 * K, 512, 2, 2) for i in range(2)]; o += 4 * K
        ZT = [A.bf("ZT%d" % i, o + i * 4 * K, 4 * 512, 2, 2).rearrange("p (c n) -> p c n", c=4) for i in range(2)]; o += 8 * K
        WA2 = [A.bf("WA2%d" % i, o + i * 4 * K, 8 * 256, 2, 2).rearrange("p (k n) -> p k n", k=8) for i in range(3)]; o += 12 * K
        MODROW2 = A.f32("MODROW2", o, 1024, 2, 2); o += 4 * K
        assert o <= RX + 60 * K, o
        YT = A.bf("YT", RY, 8 * SEQ, 2, 3).rearrange("p (k n) -> p k n", k=8)
        X = A.f32("X", RX, NT * D, 3, 4).rearrange("p (t d) -> p t d", t=NT)
        H2T = A.bf("H2T", RH, 8 * SEQ, 3, 4).rearrange("p (k n) -> p k n", k=8)
        o = RWT
        WOUT = A.bf("WOUT", o, 8 * D, 2, 3).rearrange("p (k n) -> p k n", k=8); o += 16 * K
        WOST = [A.f32("WOST%d" % i, o + i * 2 * K, 512, 2, 3) for i in range(2)]; o += 4 * K
        H2B = [A.bf("H2B%d" % i, GT1B_OFF + i * 2 * K, 1024, 3, 3) for i in range(2)]
        assert o <= LIMIT, (o, LIMIT)
        ACT_ = A.bf("A", RY, GMAX * SEQ, 4, 4).rearrange("p (j n) -> p j n", j=GMAX)
        o = RW
        WG = [A.bf("WG%d" % i, o + i * 2 * K, 8 * 128, 3, 4).rearrange("p (k n) -> p k n", k=8) for i in range(3)]; o += 6 * K
        WU = [A.bf("WU%d" % i, o + i * 2 * K, 8 * 128, 3, 4).rearrange("p (k n) -> p k n", k=8) for i in range(3)]; o += 6 * K
        WDST = [A.f32("WDST%d" % i, o + i * 4 * K, 1024, 3, 4) for i in range(2)]; o += 8 * K
        WD0 = A.bf("WD0", o, GMAX * D, 3, 4).rearrange("p (j n) -> p j n", j=GMAX); o += 12 * K
        SG = [A.f32("SG%d" % i, o + i * 2 * K, 512, 3, 4) for i in range(2)]; o += 4 * K
        assert o <= RWT, (o, RWT)
        WD1 = A.bf("WD1", RWT, GMAX * D, 4, 4).rearrange("p (j n) -> p j n", j=GMAX)
        WD = [WD0, WD1]

        S = Sched(nc)
        pbank = [0]

        busy_banks = set()

        def nb():
            while True:
                b = pbank[0]
                pbank[0] = (b + 1) % 8
                if b not in busy_banks:
                    return b

        def bankbf(b):
            return banks[b][:].bitcast(BF16).rearrange("p (k t) -> p k t", k=8)

        wa_v = w_ada.rearrange("(k p) n -> p k n", p=128)

        class AdaCtx:
            def __init__(self, was, ncols, modrow, badas, tag):
                self.was, self.ncols, self.modrow, self.badas, self.tag, self.cnt = was, ncols, modrow, badas, tag, 0
                self.bias_of = {}

        def ada_dma(ctx, s, p):
            sl = ctx.cnt % len(ctx.was)
            ctx.cnt += 1
            n = ctx.ncols
            c0 = s * 1024 + p * n
            S.add("pool", lambda e: e.dma_start(out=ctx.was[sl][:, :, :], in_=wa_v[:, :, c0:c0 + n]),
                  writes=[(ctx.tag + "WA", sl)], dma=(ctx.tag + "wa", sl))
            if ctx.tag == "2":
                ctx.bias_of[(s, p)] = None
                if p == 0:
                    S.add("sp", lambda e: e.dma_start(out=ctx.modrow[0:1, :], in_=b_ada[:, s * 1024:(s + 1) * 1024]),
                          writes=[(ctx.tag + "MODROW", q) for q in range(1024 // n)], dma=(ctx.tag + "bada", 0))
            else:
                bi = s % len(ctx.badas)
                ctx.bias_of[(s, p)] = (bi, p * n)
                if p == 0:
                    S.add("sp", lambda e: e.dma_start(out=ctx.badas[bi][0:1, :], in_=b_ada[:, s * 1024:(s + 1) * 1024]),
                          writes=[(ctx.tag + "BADA", bi)] + ([("GFROW",)] if bi == 1 else []),
                          dma=(ctx.tag + "bada", bi))
            return sl

        def m1_load(i):
            sl = i % NXIN
            S.add("sp", lambda e: e.dma_start(out=XIN[sl][:, :], in_=x_d[i * 128:(i + 1) * 128, :]),
                  writes=[("XIN", sl)], dma=("xin", sl))

        S.add("sp", lambda e: e.dma_start(out=VEC[:, 0:40], in_=vec_d[:, :]), writes=[("VEC",)], dma=("vec",))
        for i_ in range(NXIN):
            m1_load(i_)
        ctx1 = AdaCtx(WA, 256, MODROW, [BADA, BADA1], "")
        pre_sl = [ada_dma(ctx1, 0, p_) for p_ in range(4)]
        S.add("sp", lambda e: e.dma_start(out=GFROW[0:1, :], in_=gf_d[:, :]), writes=[("GFROW",)], dma=("gfrow",))
        S.add("sp", lambda e: e.dma_start(out=DWWT[:, 0:124], in_=dwwt_d[:, :]), writes=[("DWWT",)], dma=("dwwt",))
        win_v = w_in.rearrange("(k p) n -> p k n", p=128)

        def win_load(blk):
            S.add("pool", lambda e: e.dma_start(out=WIN[:, :, blk * 512:(blk + 1) * 512], in_=win_v[:, :, blk * 512:(blk + 1) * 512]),
                  writes=[("WIN", blk)], dma=("win", blk))
        S.add("pool", lambda e: e.memset(IDF, 0.0), writes=[("IDF",)])
        S.add("pool", lambda e: e.affine_select(out=IDF, in_=IDF, compare_op=ALU.not_equal, fill=1.0, base=0,
                                                 pattern=[[-1, 128]], channel_multiplier=1),
              reads=[("IDF",)], writes=[("IDF",)])
        S.add("pool", lambda e: e.tensor_copy(IDB, IDF), reads=[("IDF",)], writes=[("IDB",)])
        S.add("pool", lambda e: e.memset(ONES, 1.0), writes=[("ONES",)])
        S.add("pool", lambda e: e.memset(NEGH, -0.5), writes=[("NEGH",)])
        S.add("pool", lambda e: e.memset(EPSC, EPS), writes=[("EPSC",)])
        S.add("pool", lambda e: e.memset(SS, 0.0), writes=[("SS", w_, i_) for w_ in range(3) for i_ in range(NT)])
        S.add("pool", lambda e: e.memset(GLU[:, :, 0:30], 0.0), writes=[("GLUPAD",)])
        S.add("pool", lambda e: e.memset(VV[1][:, :, 512:528], 0.0), writes=[("V", 1, "tail")])

        def scratch_init(e):
            ins = None
            for t_ in T12 + RBUF:
                ins = e.memset(t_[:, 0:16], 0.0)
            return ins
        S.add("pool", scratch_init, writes=[("T", 0), ("T", 1)] + [("R", q) for q in range(4)])

        def cnt_fill(e):
            ins = None
            for g in range(4):
                w = 2 << g
                for t in range(16):
                    ins = e.memset(RCNT3[:, g, t:t + 1], 1.0 / min(t + 1, w))
            return ins
        S.add("pool", cnt_fill, writes=[("RCNT",)])

        S.add("act", lambda e: e.activation(CACTF[:, 0:8], VEC[:, 0:8], AF.Silu), reads=[("VEC",)], writes=[("CACTF",)])
        S.add("dve", lambda e: e.tensor_copy(CACT[:, 0:8], CACTF[:, 0:8]), reads=[("CACTF",)], writes=[("CACT",)])
        S.add("act", lambda e: e.activation(FJ[:, 24:25], EPSC[:, 0:1], AF.Sigmoid), reads=[("EPSC",)], writes=[("FJ", 24)])

        def ada_mm(ctx, s, p, sl):
            n = ctx.ncols
            b = nb()

            def mm(e):
                ins = None
                for k in range(8):
                    ins = e.matmul(banks[b][0:1, 0:n], CACT[:, k:k + 1], ctx.was[sl][:, k, :], start=(k == 0), stop=(k == 7))
                return ins
            S.add("pe", mm, reads=[("CACT",), (ctx.tag + "WA", sl)], writes=[("ps", b)])
            if ctx.bias_of[(s, p)] is None:
                S.add("dve", lambda e: e.tensor_tensor(ctx.modrow[0:1, p * n:(p + 1) * n], banks[b][0:1, 0:n],
                                                         ctx.modrow[0:1, p * n:(p + 1) * n], ALU.add),
                      reads=[("ps", b), (ctx.tag + "MODROW", p)], writes=[(ctx.tag + "MODROW", p)])
                return
            bi, boff = ctx.bias_of[(s, p)]
            S.add("dve", lambda e: e.tensor_tensor(ctx.modrow[0:1, p * n:(p + 1) * n], banks[b][0:1, 0:n],
                                                     ctx.badas[bi][0:1, boff:boff + n], ALU.add),
                  reads=[("ps", b), (ctx.tag + "BADA", bi)], writes=[(ctx.tag + "MODROW", p)])

        def row_to_cols(ctx):
            b = nb()
            MODROW = ctx.modrow

            def mm(e):
                ins = None
                for c in range(8):
                    ins = e.matmul(banks[b][:, c:c + 1], MODROW[0:1, c * 128:(c + 1) * 128], ONES[0:1, 0:1],
                                   start=True, stop=True)
                return ins
            S.add("pe", mm, reads=[(ctx.tag + "MODROW", p) for p in range(1024 // ctx.ncols)] + [("ONES",)], writes=[("ps", b)])
            return b

        def row_bcast(row, rkeys, dst, dkey):
            for hf in range(2):
                b = nb()
                S.add("pe", (lambda b, hf: lambda e: e.matmul(banks[b][:, :], ONES[0:1, :], row[0:1, hf * 512:(hf + 1) * 512],
                                                               start=True, stop=True))(b, hf),
                      reads=list(rkeys) + [("ONES",)], writes=[("ps", b)])
                S.add("dve", (lambda b, hf: lambda e: e.tensor_copy(dst[:, hf * 512:(hf + 1) * 512], banks[b][:, :]))(b, hf),
                      reads=[("ps", b)], writes=[(dkey, hf)])

        def ada_finish(ctx, s):
            MK = [(ctx.tag + "MODROW", p) for p in range(1024 // ctx.ncols)]
            if s == 0:
                b = row_to_cols(ctx)
                S.add("dve", lambda e: e.tensor_copy(AB[:, 8:16], banks[b][:, 0:8]), reads=[("ps", b)], writes=[("AB", 1)])
            elif s == 1:
                b = row_to_cols(ctx)
                S.add("dve", lambda e: e.scalar_tensor_tensor(AB[:, 0:8], banks[b][:, 0:8], 1.0, VEC[:, 8:16], ALU.add, ALU.mult),
                      reads=[("ps", b), ("VEC",)], writes=[("AB", 0)])
            elif s == 2:
                row_bcast(ctx.modrow, MK, GT1B, "GT1B")
            elif s == 3:
                b = row_to_cols(ctx)
                S.add("dve", lambda e: e.tensor_copy(AB[:, 24:32], banks[b][:, 0:8]), reads=[("ps", b)], writes=[("AB", 3)])
            elif s == 4:
                b = row_to_cols(ctx)
                S.add("dve", lambda e: e.scalar_tensor_tensor(AB[:, 16:24], banks[b][:, 0:8], 1.0, VEC[:, 16:24], ALU.add, ALU.mult),
                      reads=[("ps", b), ("VEC",)], writes=[("AB", 2)])
            else:
                row_bcast(ctx.modrow, MK, GT2B, "GT2B")

        row_bcast(GFROW, [("GFROW",)], GFB, "GFB")
        for p_ in range(4):
            ada_mm(ctx1, 0, p_, pre_sl[p_])
        ada_finish(ctx1, 0)
        sl1 = [ada_dma(ctx1, 1, p_) for p_ in range(4)]
        for blk in (1, 0, 2):
            win_load(blk)
        for p_ in range(4):
            ada_mm(ctx1, 1, p_, sl1[p_])
        ada_finish(ctx1, 1)

        def diag_build(i0, i1):
            def f(e):
                ins = None
                for idx in range(i0, i1):
                    if idx % 31 < 6:
                        continue
                    ins = e.tensor_tensor(DIAG[:, idx, :], IDF[:, :], DWWT[:, idx:idx + 1].to_broadcast([128, 128]), ALU.mult)
                return ins
            S.add("pool", f, reads=[("IDF",), ("DWWT",)], writes=[("DIAGP", i0 // 8)])
        DIAGK = [("DIAGP", q) for q in range(16)]

        def normA(i, which, src_ap, src_keys, hdst, hkey, defer_b=False):
            S.add("act", lambda e: e.activation(JUNK[:, :], src_ap, AF.Square, accum_out=SS3[:, which, i:i + 1]),
                  reads=src_keys, writes=[("JUNK",), ("SS", which, i)])
            S.add("dve", lambda e: e.tensor_scalar(RS3[:, which, i:i + 1], SS3[:, which, i:i + 1], 1.0 / D, EPS, ALU.mult, ALU.add),
                  reads=[("SS", which, i)], writes=[("RS", which, i)])
            S.add("pool", lambda e: e.tensor_tensor(RS3[:, which, i:i + 1], RS3[:, which, i:i + 1], NEGH[:, 0:1], ALU.pow),
                  reads=[("RS", which, i), ("NEGH",)], writes=[("RS", which, i)])
            def part_b():
                S.add("dve", lambda e: e.tensor_scalar(hdst, src_ap, RS3[:, which, i:i + 1], None, ALU.mult),
                      reads=src_keys + [("RS", which, i)], writes=[hkey])
            if hdst is None:
                return None
            if defer_b:
                return part_b
            part_b()
            return None

        def transposeT(hsrc, hkey, abcol, dst_fn, dkey, act_only=False):
            if act_only:
                b0_ = nb()
                bv0 = bankbf(b0_)

                def tr0(e):
                    ins = None
                    for k in range(8):
                        ins = e.transpose(bv0[:, k, :], hsrc[:, k * 128:(k + 1) * 128], IDB[:, :])
                    return ins
                S.add("pe", tr0, reads=[hkey, ("IDB",)], writes=[("ps", b0_)])

                def ev0(e):
                    ins = None
                    for k in range(8):
                        ins = e.activation(dst_fn(k), bv0[:, k, :], AF.Identity, bias=AB[:, abcol + 8 + k:abcol + 9 + k],
                                           scale=AB[:, abcol + k:abcol + k + 1])
                    return ins
                S.add("act", ev0, reads=[("ps", b0_), ("AB", abcol // 8), ("AB", abcol // 8 + 1)], writes=[dkey + ("a",), dkey + ("b",)])
                return
            b1_ = nb()
            b2_ = nb()
            bv1 = bankbf(b1_)
            bv2 = bankbf(b2_)

            def tr1(e):
                ins = None
                for k in range(0, 5):
                    ins = e.transpose(bv1[:, k, :], hsrc[:, k * 128:(k + 1) * 128], IDB[:, :])
                return ins

            def tr2(e):
                ins = None
                for k in range(5, 8):
                    ins = e.transpose(bv2[:, k, :], hsrc[:, k * 128:(k + 1) * 128], IDB[:, :])
                return ins
            S.add("pe", tr1, reads=[hkey, ("IDB",)], writes=[("ps", b1_)])
            S.add("pe", tr2, reads=[hkey, ("IDB",)], writes=[("ps", b2_)])

            def ev(e):
                ins = None
                for k in range(0, 5):
                    ins = e.activation(dst_fn(k), bv1[:, k, :], AF.Identity, bias=AB[:, abcol + 8 + k:abcol + 9 + k],
                                       scale=AB[:, abcol + k:abcol + k + 1])
                return ins

            def ev2(e):
                ins = None
                for k in range(5, 8):
                    ins = e.tensor_scalar(dst_fn(k), bv2[:, k, :], AB[:, abcol + k:abcol + k + 1], AB[:, abcol + 8 + k:abcol + 9 + k],
                                          ALU.mult, ALU.add)
                return ins
            S.add("act", ev, reads=[("ps", b1_), ("AB", abcol // 8), ("AB", abcol // 8 + 1)], writes=[dkey + ("a",)])
            S.add("dve", ev2, reads=[("ps", b2_), ("AB", abcol // 8), ("AB", abcol // 8 + 1)], writes=[dkey + ("b",)])

        def m1_normA(i):
            sl = i % 2
            xs = i % NXIN
            return normA(i, 0, XIN[xs][:, :], [("XIN", xs)], HB[sl][:, :], ("H", sl), defer_b=True)

        def m1_T(i):
            sl = i % 2
            st_, q = divmod(i, 4)
            hs = st_ % 2
            transposeT(HB[sl], ("H", sl), 0, lambda k: HT[hs][:, k, q * 128:(q + 1) * 128], ("HT", hs, q), act_only=(i >= 4))

        def p2_chunk(st_, kind, cc):
            hs = st_ % 2
            m = {"g": 4 + cc, "a": cc, "p": 8 + cc}[kind]
            b = nb()

            def mm(e):
                ins = None
                for k in range(8):
                    ins = e.matmul(banks[b][:, :], WIN[:, k, m * 128:(m + 1) * 128], HT[hs][:, k, :],
                                   start=(k == 0), stop=(k == 7))
                return ins
            S.add("pe", mm, reads=[("WIN", m // 4)] + [("HT", hs, q, h_) for q in range(4) for h_ in "ab"], writes=[("ps", b)])
            if kind == "g":
                sl = 0
                S.add("act", lambda e: e.activation(SIG[sl][:, :], banks[b][:, :], AF.Sigmoid),
                      reads=[("ps", b)], writes=[("SIG", sl)])
            elif kind == "a":
                sl = 0
                S.add("dve", lambda e: e.tensor_tensor(GLU[:, cc, 30 + st_ * 512:30 + (st_ + 1) * 512], banks[b][:, :], SIG[sl][:, :], ALU.mult),
                      reads=[("ps", b), ("SIG", sl)], writes=[("GLU", cc, st_)])
            else:
                vs = st_ % 2
                S.add("act", lambda e: e.copy(VV[vs][:, cc, 16:528], banks[b][:, :]),
                      reads=[("ps", b)], writes=[("V", vs, cc)])

        def pool_branch(st_, g):
            vs = st_ % 2
            V = VV[vs]
            Vp = VV[1 - vs]
            S.add("dve", lambda e: e.tensor_copy(V[:, g, 0:16], Vp[:, g, 512:528]),
                  reads=[("V", 1 - vs, g), ("V", 1 - vs, "tail")], writes=[("V", vs, g, "halo")])
            rk = [("V", vs, g), ("V", vs, g, "halo")]
            rb = g
            chain = [(V[:, g, :], None)]
            for step in range(g):
                chain.append((T12[step % 2][:, :], ("T", step % 2)))
            chain.append((RBUF[rb][:, :], ("R", rb)))
            sh = 1
            for step in range(g + 1):
                a_src, ak = chain[step]
                d, dk = chain[step + 1]
                S.add("dve", (lambda d, a_src, sh: lambda e: e.tensor_tensor(d[:, sh:528], a_src[:, sh:528], a_src[:, 0:528 - sh], ALU.add))(d, a_src, sh),
                      reads=rk + ([ak] if ak else []), writes=[dk])
                sh *= 2
            w = 2 << g
            sw = RBUF[rb]
            tk = ("R", rb)

            def fin():
                S.add("dve", lambda e: e.scalar_tensor_tensor(PP[:, g, st_ * 512:(st_ + 1) * 512], sw[:, 16:528], 1.0 / w, V[:, g, 16:528],
                                                              ALU.mult, ALU.subtract),
                      reads=[tk] + rk, writes=[("PP", g, st_)])
                if st_ == 0:
                    S.add("dve", lambda e: e.tensor_tensor(sw[:, 16:32], sw[:, 16:32], RCNT3[:, g, :], ALU.mult),
                          reads=[tk, ("RCNT",)], writes=[tk])
                    S.add("dve", lambda e: e.tensor_tensor(PP[:, g, 0:16], sw[:, 16:32], V[:, g, 16:32], ALU.subtract),
                          reads=[tk] + rk, writes=[("PP", g, st_)])
            return fin

        m1_normA(0)()
        pend = []
        late = []
        cur_i = [0]
        for i in range(NT):
            cur_i[0] = i
            nb_ = m1_normA(i + 1) if i + 1 < NT else None
            m1_T(i)
            if nb_:
                nb_()
            if i + NXIN < NT:
                m1_load(i + NXIN)
            for _ in range(3):
                if pend:
                    pend.pop(0)()
            while late and late[0][0] <= i:
                late.pop(0)[1]()
            if i % 4 == 3:
                st_ = i // 4
                for cc in range(4):
                    pend.append((lambda st_, cc: lambda: p2_chunk(st_, "g", cc))(st_, cc))
                    pend.append((lambda st_, cc: lambda: p2_chunk(st_, "a", cc))(st_, cc))
                for g in range(4):
                    pend.append((lambda st_, g: lambda: (p2_chunk(st_, "p", g), late.append((cur_i[0], pool_branch(st_, g)))))(st_, g))
                if st_ == 0:
                    bw = nb()

                    def warm(e):
                        ins = None
                        for q in range(28):
                            ins = e.matmul(banks[bw][:, 0:128], IDB[:, :], IDB[:, :], start=True, stop=True)
                        return ins
                    S.add("pe", warm, reads=[("IDB",)], writes=[("ps", bw)])
                for _ in range(2):
                    pend.pop(0)()
            if i * 8 < 124:
                diag_build(i * 8, min(124, i * 8 + 8))
            if i == 8:
                wpw_v = w_pw.rearrange("(k p) n -> p k n", p=128)
                S.add("pool", lambda e: e.dma_start(out=WPW[:, :, :], in_=wpw_v[:, :, :]), writes=[("WPW",)], dma=("wpw",))
                wpg_v = w_pg.rearrange("g c d -> c g d")
                S.add("pool", lambda e: e.dma_start(out=WPG[:, :, :], in_=wpg_v[:, :, :]), writes=[("WPG",)], dma=("wpg",))
        i = NT
        while pend or late:
            cur_i[0] = i
            for _ in range(3):
                if pend:
                    pend.pop(0)()
            while late and (late[0][0] <= i or not pend):
                late.pop(0)[1]()
            i += 1

        S.fence("pool", lambda e: e.memset(FJ[:, 0:2], 0.0),
                ["WIN", "HT", "XIN", "H", "SIG", "T"],
                ["CONV", "SQ", "MUB", "RSB", "TMB", "LT", "ZT", "2WA", "2MODROW", "2BADA", "XPRE"])
        S.fence("pool", lambda e: e.memset(FJ[:, 2:4], 0.0), ["V", "WA"], ["YT"])
        M4ORD = list(range(NT))
        S.add("sp", lambda e: e.dma_start(out=X[:, NT - 1, :], in_=x_d[(NT - 1) * 128:NT * 128, :]),
              writes=[("X", NT - 1, 0), ("X", NT - 1, 1), ("XPRE",)], dma=("xld", 0))
        S.fence("pool", lambda e: e.memset(FJ[:, 4:6], 0.0), ["MODROW", "BADA", "GFROW", "R"], ["WOUT", "WOST"])

        def wout_prep(k, hf):
            sl = (k * 2 + hf) % 2
            S.add("sp", lambda e: e.dma_start(out=WOST[sl][:, :], in_=w_out[k * 128:(k + 1) * 128, hf * 512:(hf + 1) * 512]),
                  writes=[("WOST", sl)], dma=("wost", sl))
            S.add("pool", lambda e: e.tensor_tensor(WOUT[:, k, hf * 512:(hf + 1) * 512], WOST[sl][:, :], GT1B[:, hf * 512:(hf + 1) * 512], ALU.mult),
                  reads=[("WOST", sl), ("GT1B", hf)], writes=[("WOUT", k, hf)])

        wprep = [(k, hf) for k in range(8) for hf in range(2)]

        NDT = 6

        def m3_taps(st_):
            cs = st_ % 2
            for j in range(NDT):
                for cc in range(4):
                    gk = [("GLUPAD",), ("GLU", cc, st_)] + ([("GLU", cc, st_ - 1)] if st_ > 0 else [])
                    if j == 0:
                        S.add("dve", (lambda cc: lambda e: e.tensor_scalar(
                            CONV[cs][:, cc, :], GLU[:, cc, st_ * 512:st_ * 512 + 512], DWWT[:, cc * 31:cc * 31 + 1],
                            VEC[:, 24 + cc:25 + cc], ALU.mult, ALU.add))(cc),
                            reads=gk + [("DWWT",), ("VEC",)], writes=[("CONV", cs, cc)])
                    else:
                        S.add("dve", (lambda cc, j: lambda e: e.scalar_tensor_tensor(
                            CONV[cs][:, cc, :], GLU[:, cc, st_ * 512 + j:st_ * 512 + j + 512],
                            DWWT[:, cc * 31 + j:cc * 31 + j + 1], CONV[cs][:, cc, :], ALU.mult, ALU.add))(cc, j),
                            reads=gk + [("DWWT",), ("CONV", cs, cc)], writes=[("CONV", cs, cc)])

        def m3_conv(st_, cc):
            cs = st_ % 2
            b = nb()
            gk = [("GLUPAD",), ("GLU", cc, st_)] + ([("GLU", cc, st_ - 1)] if st_ > 0 else [])

            def mm(e):
                ins = None
                for j in range(NDT, KW):
                    ins = e.matmul(banks[b][:, :], DIAG[:, cc * 31 + j, :], GLU[:, cc, st_ * 512 + j:st_ * 512 + j + 512],
                                   start=(j == NDT), stop=(j == KW - 1))
                return ins
            S.add("pe", mm, reads=DIAGK + gk, writes=[("ps", b)])
            S.add("dve", lambda e: e.tensor_tensor(CONV[cs][:, cc, :], CONV[cs][:, cc, :], banks[b][:, :], ALU.add),
                  reads=[("ps", b), ("CONV", cs, cc)], writes=[("CONV", cs, cc)])
            S.add("act", lambda e: e.activation(SQ[:, cc, :], CONV[cs][:, cc, :], AF.Square),
                  reads=[("CONV", cs, cc)], writes=[("SQ", cc)])

        def m3_ln(st_):
            cs = st_ % 2
            zs = st_ % 2
            b1_ = nb()
            b2_ = nb()

            def mm1(e):
                ins = None
                for cc in range(4):
                    ins = e.matmul(banks[b1_][:, :], ONES[:, :], CONV[cs][:, cc, :], start=(cc == 0), stop=(cc == 3))
                return ins

            def mm2(e):
                ins = None
                for cc in range(4):
                    ins = e.matmul(banks[b2_][:, :], ONES[:, :], SQ[:, cc, :], start=(cc == 0), stop=(cc == 3))
                return ins
            S.add("pe", mm1, reads=[("CONV", cs, cc) for cc in range(4)] + [("ONES",)], writes=[("ps", b1_)])
            S.add("pe", mm2, reads=[("SQ", cc) for cc in range(4)] + [("ONES",)], writes=[("ps", b2_)])
            S.add("dve", lambda e: e.tensor_scalar(MUB[:, :], banks[b1_][:, :], 1.0 / CW, None, ALU.mult),
                  reads=[("ps", b1_)], writes=[("MUB",)])
            S.add("dve", lambda e: e.tensor_tensor(TMB[:, :], MUB[:, :], MUB[:, :], ALU.mult), reads=[("MUB",)], writes=[("TMB",)])
            S.add("dve", lambda e: e.scalar_tensor_tensor(RSB[:, :], banks[b2_][:, :], 1.0 / CW, TMB[:, :], ALU.mult, ALU.subtract),
                  reads=[("ps", b2_), ("TMB",)], writes=[("RSB",)])
            S.add("dve", lambda e: e.tensor_scalar(RSB[:, :], RSB[:, :], 0.0, None, ALU.max), reads=[("RSB",)], writes=[("RSB",)])
            S.add("act", lambda e: e.activation(TMB[:, :], RSB[:, :], AF.Ln, bias=EPSC[:, 0:1]),
                  reads=[("RSB",), ("EPSC",)], writes=[("TMB",)])
            S.add("act", lambda e: e.activation(RSB[:, :], TMB[:, :], AF.Exp, scale=-0.5),
                  reads=[("TMB",)], writes=[("RSB",)])
            for cc in range(4):
                ls = cc % 2
                S.add("dve", (lambda cc, ls: lambda e: e.tensor_tensor(LT[ls][:, :], CONV[cs][:, cc, :], MUB[:, :], ALU.subtract))(cc, ls),
                      reads=[("CONV", cs, cc), ("MUB",)], writes=[("LT", ls)])
                S.add("dve", (lambda cc, ls: lambda e: e.tensor_tensor(LT[ls][:, :], LT[ls][:, :], RSB[:, :], ALU.mult))(cc, ls),
                      reads=[("LT", ls), ("RSB",)], writes=[("LT", ls)])
                S.add("act", (lambda cc, ls: lambda e: e.activation(ZT[zs][:, cc, :], LT[ls][:, :], AF.Silu,
                                                                     bias=VEC[:, 32 + cc:33 + cc], scale=VEC[:, 28 + cc:29 + cc]))(cc, ls),
                      reads=[("LT", ls), ("VEC",)], writes=[("ZT", zs, cc)])

        def m3_pw(st_, m):
            zs = st_ % 2
            b = nb()

            def mm(e):
                ins = None
                for cc in range(4):
                    ins = e.matmul(banks[b][:, :], WPW[:, cc, m * 128:(m + 1) * 128], ZT[zs][:, cc, :], start=(cc == 0), stop=(cc == 3))
                return ins
            S.add("pe", mm, reads=[("WPW",)] + [("ZT", zs, cc) for cc in range(4)], writes=[("ps", b)])
            S.add("act", lambda e: e.copy(YT[:, m, st_ * 512:(st_ + 1) * 512], banks[b][:, :]),
                  reads=[("ps", b)], writes=[("YT", m, st_)])

        def m3_pg(st_, g):
            b = nb()
            S.add("pe", lambda e: e.matmul(banks[b][:, :], WPG[:, g, :], PP[:, g, st_ * 512:(st_ + 1) * 512], start=True, stop=True),
                  reads=[("WPG",), ("PP", g, st_)], writes=[("ps", b)])
            S.add("act", lambda e: e.activation(YT[:, 4 + g, st_ * 512:(st_ + 1) * 512], banks[b][:, :], AF.Identity, scale=VEC[:, 36 + g:37 + g]),
                  reads=[("ps", b), ("VEC",)], writes=[("YT", 4 + g, st_)])

        def m4_out(i, defer_adds=False):
            bs = [nb(), nb()]

            def mm(e):
                ins = None
                for k in range(8):
                    for hf in range(2):
                        ins = e.matmul(banks[bs[hf]][:, :], YT[:, k, i * 128:(i + 1) * 128], WOUT[:, k, hf * 512:(hf + 1) * 512],
                                       start=(k == 0), stop=(k == 7))
                return ins
            S.add("pe", mm, reads=[("YT", k, i // 4) for k in range(8)] + [("WOUT", k, hf) for k in range(8) for hf in range(2)],
                  writes=[("ps", bs[0]), ("ps", bs[1])])
            def adds():
                busy_banks.difference_update(bs)
                for hf in range(2):
                    S.add("dve", (lambda hf: lambda e: e.tensor_tensor(X[:, i, hf * 512:(hf + 1) * 512], X[:, i, hf * 512:(hf + 1) * 512],
                                                                       banks[bs[hf]][:, :], ALU.add))(hf),
                          reads=[("ps", bs[hf]), ("X", i, hf)], writes=[("X", i, hf)])
            if defer_adds:
                busy_banks.update(bs)
                return adds
            adds()
            return None

        ctx2 = AdaCtx(WA2, 256, MODROW2, [], "2")
        m3_taps(0)
        for cc in range(4):
            m3_conv(0, cc)
        next_sls = None
        for st_ in range(NST):
            s_ = 2 + st_
            sls = next_sls if next_sls is not None else [ada_dma(ctx2, s_, p_) for p_ in range(3)]
            next_sls = None
            if st_ + 1 < NST:
                m3_taps(st_ + 1)
            m3_ln(st_)
            if st_ + 1 < NST:
                for cc in range(4):
                    m3_conv(st_ + 1, cc)
                    ada_mm(ctx2, s_, cc, sls[cc])
                    if cc == 0:
                        sls.append(ada_dma(ctx2, s_, 3))
                ada_finish(ctx2, s_)
                next_sls = [ada_dma(ctx2, s_ + 1, p_) for p_ in range(3)]
                for g in range(4):
                    m3_pg(st_, g)
            else:
                ada_mm(ctx2, s_, 0, sls[0])
                sls.append(ada_dma(ctx2, s_, 3))
                for g in range(4):
                    m3_pg(st_, g)
                ada_mm(ctx2, s_, 1, sls[1])
                ada_mm(ctx2, s_, 2, sls[2])
                early_bank0 = pbank[0]
                early_adds = [m4_out(0, defer_adds=True), m4_out(1, defer_adds=True)]
                ada_mm(ctx2, s_, 3, sls[3])
                ada_finish(ctx2, s_)
            for m in range(4):
                m3_pw(st_, m)
            for _ in range(6):
                if wprep:
                    wout_prep(*wprep.pop(0))
        while wprep:
            wout_prep(*wprep.pop(0))

        S.fence("pool", lambda e: e.memset(FJ[:, 6:8], 0.0),
                ["CONV", "SQ", "MUB", "RSB", "TMB", "LT", "ZT", "2WA", "2MODROW", "2BADA"], ["X"])
        S.fence("pool", lambda e: e.memset(FJ[:, 8:10], 0.0), ["GLU", "GLUPAD", "PP"], ["H2T"])
        S.fence("pool", lambda e: e.memset(FJ[:, 16:18], 0.0), ["GT1B"], ["H2B"])
        S.fence("pool", lambda e: e.memset(FJ[:, 10:12], 0.0), ["DIAGP", "WPW", "WPG"], ["WG", "WU", "WDST", "WD0", "SG"])

        wg_v = w_g.rearrange("(k p) n -> p k n", p=128)
        wu_v = w_u.rearrange("(k p) n -> p k n", p=128)
        gu_count = [0]

        def gu_load(j):
            ws = gu_count[0] % 3
            gu_count[0] += 1
            S.add("pool", lambda e: e.dma_start(out=WG[ws][:, :, :], in_=wg_v[:, :, j * 128:(j + 1) * 128]),
                  writes=[("WG", ws)], dma=("wg", ws))
            S.add("pool", lambda e: e.dma_start(out=WU[ws][:, :, :], in_=wu_v[:, :, j * 128:(j + 1) * 128]),
                  writes=[("WU", ws)], dma=("wu", ws))
            return ws

        wd_count = [0]

        def wd_prep(gi, jj):
            j = GROUPS[gi][jj]
            sl = wd_count[0] % 2
            wd_count[0] += 1
            S.add("sp", lambda e: e.dma_start(out=WDST[sl][:, :], in_=w_d[j * 128:(j + 1) * 128, :]),
                  writes=[("WDST", sl)], dma=("wdst", sl))
            S.add("dve", lambda e: e.tensor_tensor(WD[gi % 2][:, jj, :], WDST[sl][:, :], GT2B[:, :], ALU.mult),
                  reads=[("WDST", sl), ("GT2B", 0), ("GT2B", 1)], writes=[("WD%d" % (gi % 2), jj)])

        def m4_load(i, n):
            S.add("sp", lambda e: e.dma_start(out=X[:, i, :], in_=x_d[i * 128:(i + 1) * 128, :]),
                  writes=[("X", i, 0), ("X", i, 1)], dma=("xld", n % 6))

        def m4_norm(i, n):
            sl = n % 2
            return normA(i, 1, X[:, i, :], [("X", i, 0), ("X", i, 1)], H2B[sl][:, :], ("H2B", sl), defer_b=True)

        def m4_T(i, n):
            sl = n % 2
            transposeT(H2B[sl], ("H2B", sl), 16, lambda k: H2T[:, k, i * 128:(i + 1) * 128], ("H2T", i))

        def ffn_gu(gi, jj, j, ws, st_):
            bg = nb()
            bu = nb()

            def mmg(e):
                ins = None
                for k in range(8):
                    ins = e.matmul(banks[bg][:, :], WG[ws][:, k, :], H2T[:, k, st_ * 512:(st_ + 1) * 512], start=(k == 0), stop=(k == 7))
                return ins

            def mmu(e):
                ins = None
                for k in range(8):
                    ins = e.matmul(banks[bu][:, :], WU[ws][:, k, :], H2T[:, k, st_ * 512:(st_ + 1) * 512], start=(k == 0), stop=(k == 7))
                return ins
            hk = [("H2T", st_ * 4 + q, h_) for q in range(4) for h_ in "ab"]
            S.add("pe", mmg, reads=[("WG", ws)] + hk, writes=[("ps", bg)])
            S.add("pe", mmu, reads=[("WU", ws)] + hk, writes=[("ps", bu)])
            sl = st_ % 2
            S.add("act", lambda e: e.activation(SG[sl][:, :], banks[bg][:, :], AF.Silu), reads=[("ps", bg)], writes=[("SG", sl)])
            S.add("dve", lambda e: e.tensor_tensor(ACT_[:, jj, st_ * 512:(st_ + 1) * 512], banks[bu][:, :], SG[sl][:, :], ALU.mult),
                  reads=[("ps", bu), ("SG", sl)], writes=[("A", jj, st_)])

        done_gu = set()
        for n in range(0, 7):
            m4_load(M4ORD[n], n)
        for a_ in early_adds:
            a_()
        pbank[0] = (early_bank0 + 4) % 8
        pre = [(lambda jj: lambda: wd_prep(0, jj))(jj) for jj in range(len(GROUPS[0]))]
        gu_slots = {}
        m4_norm(M4ORD[0], 0)()
        for n in range(NT):
            if n + 2 < NT:
                m4_out(M4ORD[n + 2])
            nb_ = m4_norm(M4ORD[n + 1], n + 1) if n + 1 < NT else None
            if n == NT - 2:
                S.fence("pool", lambda e: e.memset(FJ[:, 12:14], 0.0), ["YT"], ["A"])
                for st_ in range(2):
                    ffn_gu(0, 0, GROUPS[0][0], gu_slots[GROUPS[0][0]], st_)
                    done_gu.add((0, 0, st_))
            if n == NT - 1:
                ffn_gu(0, 0, GROUPS[0][0], gu_slots[GROUPS[0][0]], 2)
                done_gu.add((0, 0, 2))
            m4_T(M4ORD[n], n)
            if nb_:
                nb_()
            if n + 7 < NT - 1:
                m4_load(M4ORD[n + 7], n + 7)
            if n >= 8 and pre:
                pre.pop(0)()
            if n == 10:
                gu_slots[GROUPS[0][0]] = gu_load(GROUPS[0][0])
            if n == 12:
                gu_slots[GROUPS[0][1]] = gu_load(GROUPS[0][1])
        while pre:
            pre.pop(0)()

        S.fence("pool", lambda e: e.memset(FJ[:, 14:16], 0.0), ["WOUT", "WOST"], ["WD1"])
        H2K = [("H2T", i) for i in range(NT)]
        all_j = [j for g in GROUPS for j in g]

        def ffn_down(gi, i):
            nj = len(GROUPS[gi])
            bs = [nb(), nb()]

            def mm(e):
                ins = None
                for jj in range(nj):
                    for hf in range(2):
                        ins = e.matmul(banks[bs[hf]][:, :], ACT_[:, jj, i * 128:(i + 1) * 128], WD[gi % 2][:, jj, hf * 512:(hf + 1) * 512],
                                       start=(jj == 0), stop=(jj == nj - 1))
                return ins
            S.add("pe", mm, reads=[("A", jj, i // 4) for jj in range(nj)] + [("WD%d" % (gi % 2), jj) for jj in range(nj)],
                  writes=[("ps", bs[0]), ("ps", bs[1])])
            for hf in range(2):
                S.add("dve", (lambda hf: lambda e: e.tensor_tensor(X[:, i, hf * 512:(hf + 1) * 512], X[:, i, hf * 512:(hf + 1) * 512],
                                                                   banks[bs[hf]][:, :], ALU.add))(hf),
                      reads=[("ps", bs[hf]), ("X", i, hf)], writes=[("X", i, hf)])

        outs = []

        def final_a(i):
            normA(i, 2, X[:, i, :], [("X", i, 0), ("X", i, 1)], None, None)

        def final_b(i):
            S.add("dve", lambda e: e.scalar_tensor_tensor(X[:, i, :], X[:, i, :], RS3[:, 2, i:i + 1], GFB[:, :], ALU.mult, ALU.mult),
                  reads=[("X", i, 0), ("X", i, 1), ("RS", 2, i), ("GFB", 0), ("GFB", 1)], writes=[("X", i, 0), ("X", i, 1)])
            outs.append(S.add("sp", lambda e: e.dma_start(out=out_d[i * 128:(i + 1) * 128, :], in_=X[:, i, :]),
                              reads=[("X", i, 0), ("X", i, 1)], dma=("out", i % 4)))

        pos = 0
        for gi, grp in enumerate(GROUPS):
            nxt = []
            if gi + 1 < len(GROUPS):
                nxt = [(lambda gi2, jj: lambda: wd_prep(gi2, jj))(gi + 1, jj) for jj in range(len(GROUPS[gi + 1]))]
            for jj, j in enumerate(grp):
                for ahead in (pos, pos + 1, pos + 2):
                    if ahead < len(all_j) and all_j[ahead] not in gu_slots:
                        gu_slots[all_j[ahead]] = gu_load(all_j[ahead])
                ws = gu_slots[j]
                for st_ in range(NST):
                    if (gi, jj, st_) not in done_gu:
                        ffn_gu(gi, jj, j, ws, st_)
                if nxt:
                    nxt.pop(0)()
                pos += 1
            while nxt:
                nxt.pop(0)()
            last = gi == len(GROUPS) - 1
            for i in range(NT):
                ffn_down(gi, i)
                if last:
                    if i >= 1:
                        final_a(i - 1)
                    if i >= 2:
                        final_b(i - 2)
            if last:
                final_a(NT - 1)
                final_b(NT - 2)
                final_b(NT - 1)

        info = S.emit(finish_ops=outs)
    return nc, info


_CACHE = {}


def kernel(**inputs):
    f = lambda a: np.ascontiguousarray(np.asarray(a, dtype=np.float32))
    x = f(inputs["x"])
    c = f(inputs["c"])
    if "nc" not in _CACHE:
        _CACHE["nc"] = build_program()[0]
    nc = _CACHE["nc"]
    T_ = lambda a, n: f(a).reshape(n, 128).T
    dw = f(inputs["dw_w"])[0]
    dwwt = np.ascontiguousarray(dw.reshape(KW, 4, 128).transpose(2, 1, 0).reshape(128, 4 * KW))
    common = [T_(inputs["g_norm1"], 8), T_(inputs["g_norm2"], 8), T_(inputs["dw_b"], 4), T_(inputs["conv_ln_g"], 4),
              T_(inputs["conv_ln_b"], 4), T_(inputs["pool_scale"], 4)]
    shared = {
        "dwwt": dwwt,
        "w_ada": f(inputs["w_ada"])[0],
        "b_ada": f(inputs["b_ada"]).reshape(1, 6 * D),
        "w_in": f(inputs["w_in"])[0],
        "w_conv_pw": f(inputs["w_conv_pw"])[0],
        "w_pool_group": f(inputs["w_pool_group"])[0],
        "w_out": f(inputs["w_out"])[0],
        "w_ffn_gate": f(inputs["w_ffn_gate"])[0],
        "w_ffn_up": f(inputs["w_ffn_up"])[0],
        "w_ffn_down": f(inputs["w_ffn_down"])[0],
        "g_final": f(inputs["g_final"]).reshape(1, D),
    }
    in_maps = []
    for b in range(8):
        m = dict(shared)
        m["x"] = x[b]
        m["vecs"] = np.ascontiguousarray(np.concatenate([c[b].reshape(8, 128).T] + common, axis=1))
        in_maps.append(m)
    res = run_bass_kernel_spmd(nc, in_maps, core_ids=list(range(8)))
    return np.stack([np.asarray(r["out"], dtype=np.float32) for r in res.results], axis=0)
```
